# Optimizing a Trainium2 kernel written in Bass

```python
import math
import jax, jax.numpy as jnp
from jax import lax
import numpy as np

D_MODEL = 2048
BATCH = 4
SEQ = 4096
DEPTH = 2

CHUNK = 64
N_A = DEPTH // 2
N_B = DEPTH - N_A
POOL_WINDOWS = (2, 4, 8, 16)
N_POOL_GROUPS = len(POOL_WINDOWS)
POOL_GROUP_DIM = D_MODEL // N_POOL_GROUPS
HEAD_DIM = 128
N_HEADS = D_MODEL // HEAD_DIM
D_FF = 4 * D_MODEL
Q_BLOCK = 128
EPS = 1e-6

kernel_name = "yoco_pool_stickbreaking_trunk"


def rms_norm(x, g):
    xf = x.astype(jnp.float32)
    y = xf * lax.rsqrt(jnp.mean(xf * xf, axis=-1, keepdims=True) + EPS)
    return (y * g.astype(jnp.float32)).astype(x.dtype)


def multiscale_pool_mixer(x, w_pool, scale):
    B, S, D = x.shape
    xg = x.reshape(B, S, N_POOL_GROUPS, POOL_GROUP_DIM)
    pos = jnp.arange(S)
    outs = []
    for g, w in enumerate(POOL_WINDOWS):
        xf = xg[:, :, g, :].astype(jnp.float32)
        cs = jnp.cumsum(xf, axis=1)
        lag = jnp.pad(cs, ((0, 0), (w, 0), (0, 0)))[:, :S]
        cnt = jnp.minimum(pos + 1, w).astype(jnp.float32)[None, :, None]
        diff = ((cs - lag) / cnt - xf).astype(x.dtype)
        outs.append(jnp.einsum('bsc,ce->bse', diff, w_pool[g]))
    return jnp.concatenate(outs, axis=-1) * scale


def squared_relu_mlp(x, w_up, w_down):
    h = jax.nn.relu(jnp.einsum('bsd,df->bsf', x, w_up))
    return jnp.einsum('bsf,fd->bsd', h * h, w_down)


def stick_breaking_attention(q, k, v):
    B, S, H, Dh = q.shape
    inv_sqrt = 1.0 / math.sqrt(Dh)
    outs = []
    for i in range(S // Q_BLOCK):
        q0 = i * Q_BLOCK
        end = q0 + Q_BLOCK
        qb = q[:, q0:end]
        kb = k[:, :end]
        vb = v[:, :end]
        z = jnp.einsum('bthd,bshd->bhts', qb, kb).astype(jnp.float32) * inv_sqrt
        t_idx = q0 + jnp.arange(Q_BLOCK)
        s_idx = jnp.arange(end)
        causal = s_idx[None, :] < t_idx[:, None]
        log_beta = jax.nn.log_sigmoid(z)
        log_1m = jnp.where(causal, jax.nn.log_sigmoid(-z), 0.0)
        between = lax.cumsum(log_1m, axis=3, reverse=True) - log_1m
        a = jnp.where(causal, jnp.exp(log_beta + between), 0.0)
        outs.append(jnp.einsum('bhts,bshd->bthd', a.astype(vb.dtype), vb))
    return jnp.concatenate(outs, axis=1)


def setup_inputs(seed: int = 0) -> dict:
    key = jax.random.key(seed)
    ks = jax.random.split(key, 16)
    f32 = jnp.float32

    def nrm(k, shape, fan_in):
        return jax.random.normal(k, shape, f32) * (fan_in ** -0.5)

    def gain(k, shape):
        return 1.0 + 0.05 * jax.random.normal(k, shape, f32)

    x = jax.random.normal(ks[0], (BATCH, SEQ, D_MODEL), f32)
    pool_norm = gain(ks[1], (N_A, D_MODEL))
    pool_w = nrm(ks[2], (N_A, N_POOL_GROUPS, POOL_GROUP_DIM, POOL_GROUP_DIM), POOL_GROUP_DIM)
    pool_scale = 0.5 + 0.05 * jax.random.normal(ks[3], (N_A, D_MODEL), f32)
    kv_norm = gain(ks[4], (D_MODEL,))
    w_kv = nrm(ks[5], (D_MODEL, 2 * D_MODEL), D_MODEL)
    attn_norm = gain(ks[6], (N_B, D_MODEL))
    w_q = nrm(ks[7], (N_B, D_MODEL, D_MODEL), D_MODEL)
    w_o = nrm(ks[8], (N_B, D_MODEL, D_MODEL), D_MODEL)
    mlp_norm = gain(ks[9], (DEPTH, D_MODEL))
    w_up = nrm(ks[10], (DEPTH, D_MODEL, D_FF), D_MODEL)
    w_down = nrm(ks[11], (DEPTH, D_FF, D_MODEL), D_FF)
    final_norm = gain(ks[12], (D_MODEL,))
    return {"x": x, "pool_norm": pool_norm, "pool_w": pool_w, "pool_scale": pool_scale,
            "kv_norm": kv_norm, "w_kv": w_kv, "attn_norm": attn_norm, "w_q": w_q,
            "w_o": w_o, "mlp_norm": mlp_norm, "w_up": w_up, "w_down": w_down,
            "final_norm": final_norm}


def reference(x, pool_norm, pool_w, pool_scale, kv_norm, w_kv, attn_norm, w_q, w_o,
              mlp_norm, w_up, w_down, final_norm):
    B, S, D = x.shape
    k_shared = None
    v_shared = None
    for layer in range(DEPTH):
        if layer < N_A:
            x = x + multiscale_pool_mixer(rms_norm(x, pool_norm[layer]), pool_w[layer],
                                          pool_scale[layer])
        else:
            if layer == N_A:
                kv = jnp.einsum('bsd,de->bse', rms_norm(x, kv_norm), w_kv)
                kv = kv.reshape(B, S, 2, N_HEADS, HEAD_DIM)
                k_shared = kv[:, :, 0]
                v_shared = kv[:, :, 1]
            j = layer - N_A
            q = jnp.einsum('bsd,de->bse', rms_norm(x, attn_norm[j]), w_q[j])
            q = q.reshape(B, S, N_HEADS, HEAD_DIM)
            o = stick_breaking_attention(q, k_shared, v_shared).reshape(B, S, D)
            x = x + jnp.einsum('bsd,de->bse', o, w_o[j])
        x = x + squared_relu_mlp(rms_norm(x, mlp_norm[layer]), w_up[layer], w_down[layer])
    return rms_norm(x, final_norm)
```

```python
import contextlib
import numpy as np
import ml_dtypes
import concourse.bass as bass
import concourse.mybir as mybir
from concourse.bass_utils import run_bass_kernel_spmd

F32 = mybir.dt.float32
BF16 = mybir.dt.bfloat16
AF = mybir.ActivationFunctionType
ALU = mybir.AluOpType

D = 2048
KC = 16
T = 512
NT = 4
HALO = 16
TW = T + HALO
DFF = 8192
H = 16
SEQ = 4096
EPS = 1e-6
WINS = (2, 4, 8, 16)
NS = 4
NB = 4
LA = NB - 1
SLAB = 2048
INV_SQRT = 1.0 / float(np.sqrt(128.0))
V_PN, V_PS, V_KV, V_AN, V_M0, V_M1, V_FN = [16 * k for k in range(7)]


class Op:
    __slots__ = ("eng", "fn", "deps", "dma", "slot", "ev", "needed", "inc")

    def __init__(self, eng, fn, dma, slot, inc=16):
        self.eng = eng
        self.fn = fn
        self.dma = dma
        self.slot = slot
        self.inc = inc
        self.deps = ()
        self.ev = None
        self.needed = False


class Prog:
    ENGS = ("sync", "act", "dve", "pool", "pe")

    def __init__(self, nc):
        self.nc = nc
        self.ops = {e: [] for e in self.ENGS}
        self.last_w = {}
        self.readers = {}
        self.finals = []
        self.pending = {e: set() for e in self.ENGS}
        self.dma_since = []

    def retire(self, keys):
        ops = set()
        for k in keys:
            w = self.last_w.get(k)
            if w is not None:
                ops.add(w)
            ops.update(self.readers.get(k, ()))
        return ops

    def seed(self, key, ops):
        self.readers.setdefault(key, []).extend(ops)

    def add(self, eng, fn, reads=(), writes=(), dma=False, slot=None, inc=16):
        op = Op(eng, fn, dma, slot, inc)
        deps = set()
        for k in reads:
            w = self.last_w.get(k)
            if w is not None:
                deps.add(w)
        for k in writes:
            w = self.last_w.get(k)
            if w is not None:
                deps.add(w)
            for r in self.readers.get(k, ()):
                deps.add(r)
        if self.pending[eng]:
            deps |= self.pending[eng]
            self.pending[eng] = set()
        if eng == "pe" and not dma:
            deps = {d for d in deps if not (d.eng == "pe" and not d.dma)}
        op.deps = deps
        for d in deps:
            d.needed = True
        for k in reads:
            self.readers.setdefault(k, []).append(op)
        for k in writes:
            self.last_w[k] = op
            self.readers[k] = []
        self.ops[eng].append(op)
        if dma:
            self.dma_since.append(op)
        return op

    def barrier(self):
        deps = set(self.dma_since)
        for e in ("act", "dve", "pool", "pe"):
            for op in reversed(self.ops[e]):
                if not op.dma:
                    deps.add(op)
                    break
        for e in self.ENGS:
            self.pending[e] |= deps
        self.dma_since = []

    def final_wait(self, eng, ops):
        for o in ops:
            o.needed = True
        self.finals.append((eng, list(ops)))

    def emit(self):
        nc = self.nc
        slots = []
        for e in self.ENGS:
            for op in self.ops[e]:
                if op.dma and op.slot not in slots:
                    slots.append(op.slot)
        with contextlib.ExitStack() as st:
            esem = {}
            for e in ("act", "dve", "pool", "pe"):
                esem[e] = st.enter_context(nc.semaphore("s_" + e))
            ssem = {}
            for i, s in enumerate(slots):
                ssem[s] = st.enter_context(nc.semaphore("d_%d" % i))
            cnt = {e: 0 for e in esem}
            scnt = {s: 0 for s in slots}
            for e in self.ENGS:
                for op in self.ops[e]:
                    if op.dma:
                        scnt[op.slot] += op.inc
                        op.ev = (ssem[op.slot], scnt[op.slot])
                    elif op.needed:
                        cnt[e] += 1
                        op.ev = (esem[e], cnt[e])
            block = st.enter_context(nc.Block())
            handles = {
                "sync": block.sync,
                "act": block.scalar,
                "dve": block.vector,
                "pool": block.gpsimd,
                "pe": block.tensor,
            }

            def make(e):
                def body(h):
                    waited = {}

                    def wait_for(d):
                        sem, val = d.ev
                        if waited.get(id(sem), 0) < val:
                            h.wait_ge(sem, val)
                            waited[id(sem)] = val

                    for op in self.ops[e]:
                        for d in op.deps:
                            wait_for(d)
                        ins = op.fn(h)
                        if op.dma:
                            ins.then_inc(op.ev[0], op.inc)
                        elif op.needed:
                            ins.then_inc(op.ev[0], 1)
                    for (fe, fops) in self.finals:
                        if fe == e:
                            for d in fops:
                                wait_for(d)

                return body

            for e in self.ENGS:
                handles[e](make(e))


def build():
    nc = bass.Bass("TRN2", target_bir_lowering=False)
    P = Prog(nc)

    def din(name, shape, dt=F32):
        return nc.dram_tensor(name, list(shape), dt, kind="ExternalInput").ap()

    def dout(name, shape, dt=F32):
        return nc.dram_tensor(name, list(shape), dt, kind="ExternalOutput").ap()

    def dint(name, shape, dt=F32):
        return nc.dram_tensor(name, list(shape), dt)

    vecs_d = din("vecs", [128, 7 * 16])
    cst_d = din("cst", [128, 4 * 128], BF16)
    w_up_d = din("w_up", [2, D, DFF])
    w_down_d = din("w_down", [2, DFF, D])
    xT_d = din("xT", [D, NT, TW])
    invcnt_d = din("invcnt", [128, 4 * T])
    pool_w_d = din("pool_w", [4, 512, 512])
    w_kv_d = din("w_kv", [D, 2 * D])
    negm_d = din("negm", [128, 8 * T], BF16)
    w_q_d = din("w_q", [D, D])
    w_o_d = din("w_o", [D, D])
    outT_d = dout("outT", [D, NT, T])
    x1T_t = dint("x1T", [D, NT, T])
    kmine_t = dint("kmine", [NT, 4, 512, 512], BF16)
    vmine_t = dint("vmine", [NT, 4, 512, 512], BF16)
    kall_t = dint("kall", [NT, 4, 1024, 512], BF16)
    vall_t = dint("vall", [NT, 4, 1024, 512], BF16)
    qsp_t = dint("qsp", [NT, 4, 512, 512], BF16)
    x1T_d = x1T_t.ap()

    st = contextlib.ExitStack()
    with st:
        def sb(name, shape, dt):
            return st.enter_context(nc.sbuf_tensor(name, list(shape), dt))

        X = sb("X", [128, KC * TW], F32)
        XN = sb("XN", [128, KC * T], BF16)
        G = sb("G", [128, 32 * T], BF16)
        RSTD = sb("RSTD", [128, TW], F32)
        SQ = [sb("SQ%d" % k, [128, TW], BF16) for k in range(2)]
        RL = [sb("RL%d" % k, [128, T], BF16) for k in range(2)]
        STG = [sb("STG%d" % k, [128, SLAB], F32) for k in range(NS)]
        WB = [sb("WB%d" % k, [128, SLAB], BF16) for k in range(NB)]
        CST = sb("CST", [128, 4 * 128], BF16)
        VECS = sb("VECS", [128, 7 * 16], F32)
        EPSB = sb("EPSB", [128, 1], F32)
        PHN = 26624
        PH = sb("PH", [128, PHN], BF16)
        PHf = PH[:].bitcast(F32)

        def carve_bf(off, n):
            return PH[:, off:off + n]

        def carve_f32(off, n):
            return PHf[:, off // 2:(off + n) // 2]

        IC = carve_f32(0, 4096)
        TMPv = [carve_f32(4096 + k * 1056, 1056) for k in range(4)]
        KST = [carve_bf(8320 + k * 2048, 2048) for k in range(2)]
        VST = [carve_bf(12416 + k * 2048, 2048) for k in range(2)]
        NEGM = carve_bf(0, 4096)
        QT = carve_bf(4096, 8192)
        OT = carve_bf(12288, 8192)
        EB = [carve_f32(20480 + k * 1024, 1024) for k in range(2)]
        SPB = [carve_bf(22528 + k * 512, 512) for k in range(3)]
        AB = [carve_bf(24064 + k * 512, 512) for k in range(3)]
        SB_ = [carve_bf(25600 + k * 512, 512) for k in range(2)]
        PS = [st.enter_context(nc.psum_tensor("PS%d" % k, [128, T], F32)) for k in range(8)]

        Xv = X[:].rearrange("p (k c) -> p k c", k=KC)
        XNv = XN[:].rearrange("p (k c) -> p k c", k=KC)
        Gv = G[:].rearrange("p (k c) -> p k c", k=32)
        ident = CST[:, 0:128]
        negU = CST[:, 128:256]
        negones = CST[:, 256:384]
        meanones = CST[:, 384:512]

        cnt = {"bank": 0, "slab": 0, "sq": 0, "rl": 0}

        def bank():
            b = cnt["bank"] % 8
            cnt["bank"] += 1
            return PS[b], ("ps", b)

        plan = []
        state = {"dry": True, "issued": 0}

        def issue_slab(n):
            src3, R, W = plan[n]
            s = n % NS
            b = n % NB
            stv = STG[s][:, 0:R * W]
            P.add("sync", lambda h: h.dma_start(out=stv.rearrange("p (r w) -> p r w", r=R), in_=src3),
                  writes=[("st", s)], dma=True, slot=("st", s))
            wbv = WB[b][:, 0:R * W]
            if n % 2 == 0:
                P.add("dve", lambda h: h.tensor_copy(out=wbv, in_=stv), reads=[("st", s)], writes=[("wb", b)])
            else:
                P.add("act", lambda h: h.activation(out=wbv, in_=stv, func=AF.Copy), reads=[("st", s)],
                      writes=[("wb", b)])

        def load_slab(src3, R, W):
            n = cnt["slab"]
            cnt["slab"] += 1
            b = n % NB
            if state["dry"]:
                plan.append((src3, R, W))
            else:
                while state["issued"] <= min(n + LA, len(plan) - 1):
                    issue_slab(state["issued"])
                    state["issued"] += 1
            return WB[b][:, 0:R * W].rearrange("p (r w) -> p r w", r=R), ("wb", b)

        def wslab(w2d, r0, c0, R, W):
            return w2d[r0:r0 + R * 128, c0:c0 + W].rearrange("(r p) w -> p r w", p=128)

        def proj_fm(w2d, row0, col0, kchunks, ngroups, rhs_fn, rhs_key, evac_fn):
            for og in range(ngroups):
                banks = [bank() for _ in range(4)]
                for kq in range(kchunks // 4):
                    wv, wk = load_slab(wslab(w2d, row0 + kq * 512, col0 + og * 512, 4, 512), 4, 512)
                    for kl in range(4):
                        kc = kq * 4 + kl
                        for ol in range(4):
                            pb, pk = banks[ol]
                            P.add("pe", lambda h, wv=wv, pb=pb, kl=kl, ol=ol, kc=kc: h.matmul(
                                pb[:], lhsT=wv[:, kl, ol * 128:(ol + 1) * 128], rhs=rhs_fn(kc),
                                start=(kc == 0), stop=(kc == kchunks - 1)),
                                  reads=[wk, rhs_key(kc)], writes=[pk])
                for ol in range(4):
                    evac_fn(og * 4 + ol, banks[ol][0], banks[ol][1])

        def rms_stats(c0, ncol):
            segs = []
            c = c0
            if c0 < HALO:
                segs.append((c0, HALO))
                c = HALO
            segs.append((c, c0 + ncol))
            banks = [bank() for _ in segs]
            for kc in range(KC):
                j = cnt["sq"] % 2
                cnt["sq"] += 1
                sq = SQ[j]
                P.add("act", lambda h, sq=sq, kc=kc: h.activation(out=sq[:, c0:c0 + ncol], in_=Xv[:, kc, c0:c0 + ncol],
                                                                  func=AF.Square),
                      reads=[("x", kc)], writes=[("sq", j)])
                for (a, b_), (pb, pk) in zip(segs, banks):
                    P.add("pe", lambda h, sq=sq, a=a, b_=b_, pb=pb, kc=kc: h.matmul(
                        pb[:, 0:b_ - a], lhsT=meanones, rhs=sq[:, a:b_], start=(kc == 0), stop=(kc == KC - 1)),
                          reads=[("sq", j), "cst"], writes=[pk])
            for (a, b_), (pb, pk) in zip(segs, banks):
                P.add("act", lambda h, a=a, b_=b_, pb=pb: h.activation(out=RSTD[:, a:b_], in_=pb[:, 0:b_ - a],
                                                                       func=AF.Sqrt, bias=EPSB[:], scale=1.0),
                      reads=[pk, "eps"], writes=["rstd"])
            P.add("dve", lambda h: h.reciprocal(out=RSTD[:, c0:c0 + ncol], in_=RSTD[:, c0:c0 + ncol]),
                  reads=["rstd"], writes=["rstd"])

        def norm_to_xn(vcol):
            rms_stats(HALO, T)
            for kc in range(KC):
                P.add("dve", lambda h, kc=kc: h.scalar_tensor_tensor(
                    out=XNv[:, kc, :], in0=Xv[:, kc, HALO:TW], scalar=VECS[:, vcol + kc:vcol + kc + 1],
                    in1=RSTD[:, HALO:TW], op0=ALU.mult, op1=ALU.mult),
                      reads=[("x", kc), "rstd", "vecs"], writes=[("xn", kc)])

        def mlp(layer, vcol):
            norm_to_xn(vcol)
            wu = w_up_d[layer]
            wd = w_down_d[layer]
            for fh in range(2):
                def evac_h(fl, pb, pk):
                    j = cnt["rl"] % 2
                    cnt["rl"] += 1
                    rl = RL[j]
                    P.add("act", lambda h, rl=rl, pb=pb: h.activation(out=rl[:], in_=pb[:], func=AF.Relu),
                          reads=[pk], writes=[("rl", j)])
                    P.add("dve", lambda h, rl=rl, fl=fl: h.tensor_tensor(out=Gv[:, fl, :], in0=rl[:], in1=rl[:],
                                                                         op=ALU.mult),
                          reads=[("rl", j)], writes=[("g", fl)])

                proj_fm(wu, 0, fh * 4096, KC, 8, lambda kc: XNv[:, kc, :], lambda kc: ("xn", kc), evac_h)

                def evac_y(dc, pb, pk):
                    P.add("dve", lambda h, pb=pb, dc=dc: h.tensor_tensor(
                        out=Xv[:, dc, HALO:TW], in0=pb[:], in1=Xv[:, dc, HALO:TW], op=ALU.add),
                          reads=[pk, ("x", dc)], writes=[("x", dc)])

                proj_fm(wd, fh * 4096, 0, 32, 4, lambda fc: Gv[:, fc, :], lambda fc: ("g", fc), evac_y)

        def construct():
            finals = []
            P.add("sync", lambda h: h.dma_start(out=CST[:], in_=cst_d), writes=["cst"], dma=True, slot="cst")
            P.add("sync", lambda h: h.dma_start(out=VECS[:], in_=vecs_d), writes=["vecs"], dma=True, slot="vecs")
            P.add("dve", lambda h: h.memset(EPSB[:], EPS), writes=["eps"])
            xkeys = [("x", kc) for kc in range(KC)]

            P.add("sync", lambda h: h.dma_start(out=IC, in_=invcnt_d), writes=["ic"], dma=True, slot="ic")
            ICv = IC.rearrange("p (g c) -> p g c", g=4)
            xsrc = xT_d.rearrange("(k p) i c -> p k i c", p=128)
            x1v = x1T_d.rearrange("(k p) i c -> p k i c", p=128)
            GROUPS = [[0, 1], [2, 3], [4, 5], [6, 7]]
            for i in range(NT):
                P.add("sync", lambda h, i=i: h.dma_start(out=Xv, in_=xsrc[:, :, i, :]),
                      writes=xkeys, dma=True, slot="x")
                rms_stats(0, TW)
                for kc in range(KC):
                    g = kc // 4
                    w = WINS[g]
                    e = "dve"
                    xnp = TMPv[0]
                    P.add(e, lambda h, kc=kc, xnp=xnp: h.scalar_tensor_tensor(
                        out=xnp, in0=Xv[:, kc, :], scalar=VECS[:, V_PN + kc:V_PN + kc + 1], in1=RSTD[:],
                        op0=ALU.mult, op1=ALU.mult),
                          reads=[("x", kc), "rstd", "vecs"], writes=[("tmp", 0)])
                    cur, curk = xnp, ("tmp", 0)
                    m = 2
                    lvl = 0
                    while m <= w:
                        dst = TMPv[1 + lvl % 2]
                        dk = ("tmp", 1 + lvl % 2)
                        hm = m // 2
                        P.add(e, lambda h, dst=dst, cur=cur, m=m, hm=hm: h.tensor_tensor(
                            out=dst[:, m - 1:TW], in0=cur[:, m - 1:TW], in1=cur[:, m - 1 - hm:TW - hm], op=ALU.add),
                              reads=[curk], writes=[dk])
                        cur, curk = dst, dk
                        m *= 2
                        lvl += 1
                    if i == 0:
                        t3 = TMPv[3]
                        P.add(e, lambda h, t3=t3, cur=cur, g=g: h.tensor_tensor(
                            out=t3[:, 0:T], in0=cur[:, HALO:TW], in1=ICv[:, g, :], op=ALU.mult),
                              reads=[curk, "ic"], writes=[("tmp", 3)])
                        P.add(e, lambda h, t3=t3, xnp=xnp, kc=kc: h.tensor_tensor(
                            out=XNv[:, kc, :], in0=t3[:, 0:T], in1=xnp[:, HALO:TW], op=ALU.subtract),
                              reads=[("tmp", 3), ("tmp", 0)], writes=[("xn", kc)])
                    else:
                        P.add(e, lambda h, cur=cur, xnp=xnp, kc=kc, w=w: h.scalar_tensor_tensor(
                            out=XNv[:, kc, :], in0=cur[:, HALO:TW], scalar=1.0 / w, in1=xnp[:, HALO:TW],
                            op0=ALU.mult, op1=ALU.subtract),
                              reads=[curk, ("tmp", 0)], writes=[("xn", kc)])
                for g in range(4):
                    def evac_p(ec, pb, pk, g=g):
                        dc = 4 * g + ec
                        P.add("dve", lambda h, pb=pb, dc=dc: h.scalar_tensor_tensor(
                            out=Xv[:, dc, HALO:TW], in0=pb[:], scalar=VECS[:, V_PS + dc:V_PS + dc + 1],
                            in1=Xv[:, dc, HALO:TW], op0=ALU.mult, op1=ALU.add),
                              reads=[pk, ("x", dc), "vecs"], writes=[("x", dc)])

                    proj_fm(pool_w_d[g], 0, 0, 4, 1, lambda cc, g=g: XNv[:, 4 * g + cc, :],
                            lambda cc, g=g: ("xn", 4 * g + cc), evac_p)
                mlp(0, V_M0)
                P.add("act", lambda h, i=i: h.dma_start(out=x1v[:, :, i, :], in_=Xv[:, :, HALO:TW]),
                      reads=xkeys, writes=[("x1d", i)], dma=True, slot="x1")
                norm_to_xn(V_KV)
                for hg in range(4):
                    kst = KST[hg % 2]
                    kstv = kst.rearrange("p (h c) -> p h c", h=4)

                    def evac_k(hh, pb, pk, kstv=kstv, hg=hg):
                        hl = hh % 4
                        P.add("act", lambda h, pb=pb, hl=hl: h.activation(out=kstv[:, hl, :], in_=pb[:], func=AF.Copy),
                              reads=[pk], writes=[("kst", hg % 2)])

                    def proj_k(hg=hg, evac_k=evac_k):
                        banks = [bank() for _ in range(4)]
                        for kq in range(4):
                            wv, wk = load_slab(wslab(w_kv_d, kq * 512, hg * 512, 4, 512), 4, 512)
                            for kl in range(4):
                                kc = kq * 4 + kl
                                for ol in range(4):
                                    pb, pk = banks[ol]
                                    P.add("pe", lambda h, wv=wv, pb=pb, kl=kl, ol=ol, kc=kc: h.matmul(
                                        pb[:], lhsT=wv[:, kl, ol * 128:(ol + 1) * 128], rhs=XNv[:, kc, :],
                                        start=(kc == 0), stop=(kc == KC - 1)),
                                          reads=[wk, ("xn", kc)], writes=[pk])
                        for ol in range(4):
                            evac_k(hg * 4 + ol, banks[ol][0], banks[ol][1])

                    proj_k()
                    dst = kmine_t[i, hg].rearrange("(h p) c -> p h c", p=128)
                    P.add("act", lambda h, dst=dst, kstv=kstv: h.dma_start(out=dst, in_=kstv),
                          reads=[("kst", hg % 2)], writes=[("kmine", i, hg)], dma=True, slot=("kst", hg % 2))
                    P.add("pool", lambda h, i=i, hg=hg: h.collective_compute(
                        "AllGather", ALU.bypass, replica_groups=GROUPS, ins=[kmine_t[i, hg]], outs=[kall_t[i, hg]]),
                          reads=[("kmine", i, hg)], writes=[("kall", i, hg)], dma=True, slot="cc", inc=1)
                for hg in range(4):
                    vst = VST[hg % 2]
                    vstv = vst.rearrange("p (h t d) -> p h t d", h=4, t=4)
                    banks = [bank() for _ in range(4)]
                    for kq in range(4):
                        wv, wk = load_slab(wslab(w_kv_d, kq * 512, D + hg * 512, 4, 512), 4, 512)
                        for kl in range(4):
                            kc = kq * 4 + kl
                            for tt in range(4):
                                pb, pk = banks[tt]
                                P.add("pe", lambda h, wv=wv, pb=pb, kl=kl, kc=kc, tt=tt: h.matmul(
                                    pb[:], lhsT=XNv[:, kc, tt * 128:(tt + 1) * 128], rhs=wv[:, kl, :],
                                    start=(kc == 0), stop=(kc == KC - 1)),
                                      reads=[wk, ("xn", kc)], writes=[pk])
                    for tt in range(4):
                        pb, pk = banks[tt]
                        P.add("dve", lambda h, pb=pb, vstv=vstv, tt=tt: h.tensor_copy(
                            out=vstv[:, :, tt, :], in_=pb[:].rearrange("p (h d) -> p h d", h=4)),
                              reads=[pk], writes=[("vst", hg % 2)])
                    dst = vmine_t[i, hg].rearrange("(h p) (t d) -> p h t d", p=128, t=4)
                    P.add("act", lambda h, dst=dst, vstv=vstv: h.dma_start(out=dst, in_=vstv),
                          reads=[("vst", hg % 2)], writes=[("vmine", i, hg)], dma=True, slot=("vst", hg % 2))
                    P.add("pool", lambda h, i=i, hg=hg: h.collective_compute(
                        "AllGather", ALU.bypass, replica_groups=GROUPS, ins=[vmine_t[i, hg]], outs=[vall_t[i, hg]]),
                          reads=[("vmine", i, hg)], writes=[("vall", i, hg)], dma=True, slot="cc", inc=1)

                norm_to_xn(V_AN)
                for hg in range(4):
                    kst = KST[hg % 2]
                    kstv = kst.rearrange("p (h c) -> p h c", h=4)
                    banks = [bank() for _ in range(4)]
                    for kq in range(4):
                        wv, wk = load_slab(wslab(w_q_d, kq * 512, hg * 512, 4, 512), 4, 512)
                        for kl in range(4):
                            kc = kq * 4 + kl
                            for ol in range(4):
                                pb, pk = banks[ol]
                                P.add("pe", lambda h, wv=wv, pb=pb, kl=kl, ol=ol, kc=kc: h.matmul(
                                    pb[:], lhsT=wv[:, kl, ol * 128:(ol + 1) * 128], rhs=XNv[:, kc, :],
                                    start=(kc == 0), stop=(kc == KC - 1)),
                                      reads=[wk, ("xn", kc)], writes=[pk])
                    for ol in range(4):
                        pb, pk = banks[ol]
                        P.add("act", lambda h, pb=pb, ol=ol, kstv=kstv: h.activation(
                            out=kstv[:, ol, :], in_=pb[:], func=AF.Copy, scale=INV_SQRT),
                              reads=[pk], writes=[("kst", hg % 2)])
                    dst = qsp_t[i, hg].rearrange("(h p) c -> p h c", p=128)
                    P.add("act", lambda h, dst=dst, kstv=kstv: h.dma_start(out=dst, in_=kstv),
                          reads=[("kst", hg % 2)], writes=[("qsp", i, hg)], dma=True, slot=("kst", hg % 2))

            a_ops = P.retire(["ic", ("tmp", 0), ("tmp", 1), ("tmp", 2), ("tmp", 3), ("kst", 0), ("kst", 1),
                              ("vst", 0), ("vst", 1)])
            bkeys = (["negm"] + [("q", h_) for h_ in range(H)] + [("o", h_) for h_ in range(H)]
                     + [("eb", k) for k in range(2)] + [("sp", k) for k in range(3)] + [("ab", k) for k in range(3)]
                     + [("s", k) for k in range(2)])
            for k in bkeys:
                P.seed(k, a_ops)

            P.add("sync", lambda h: h.dma_start(out=NEGM, in_=negm_d), writes=["negm"], dma=True, slot="negm")
            NEGMv = NEGM.rearrange("p (m c) -> p m c", m=8)
            QTv = QT.rearrange("p (h c) -> p h c", h=H)
            OTv = OT.rearrange("p (h c) -> p h c", h=H)
            odst = outT_d.rearrange("(k p) i c -> p k i c", p=128)
            KTB = [G[:, hb * 8192:hb * 8192 + 4096] for hb in range(2)]
            VB = [G[:, hb * 8192 + 4096:hb * 8192 + 8192].rearrange("p (t d) -> p t d", d=128) for hb in range(2)]
            gkeys = [("g", k) for k in range(32)]

            def kv_load(i, hh):
                hb = hh % 2
                hg, hl = hh // 4, hh % 4
                nch = i + 1
                for r in range(2):
                    r0 = r * 512 + hl * 128
                    ktd = KTB[hb].rearrange("p (c r t) -> p c r t", r=2, t=T)[:, 0:nch, r, :]
                    kts = kall_t[0:nch, hg, r0:r0 + 128, :].rearrange("i p c -> p i c")
                    P.add("sync", lambda h, ktd=ktd, kts=kts: h.dma_start(out=ktd, in_=kts),
                          reads=[("kall", i2, hg) for i2 in range(nch)],
                          writes=[("ktb", hb)] + gkeys[hb * 16:hb * 16 + 8], dma=True, slot=("ktb", hb))
                    vd = VB[hb].rearrange("p (c r t) d -> p c r t d", r=2, t=4)[:, 0:nch, r, :, :]
                    vs = vall_t[0:nch, hg, r0:r0 + 128, :].rearrange("i p (t d) -> p i t d", t=4)
                    P.add("sync", lambda h, vd=vd, vs=vs: h.dma_start(out=vd, in_=vs),
                          reads=[("vall", i2, hg) for i2 in range(nch)],
                          writes=[("vb", hb)] + gkeys[hb * 16 + 8:hb * 16 + 16], dma=True, slot=("vb", hb))

            def attention(i):
                nk = 8 * i + 8
                pairs = [(hh, kt) for hh in range(H) for kt in range(nk - 1, -1, -1)]
                n = len(pairs)
                kv_load(i, 0)
                kv_load(i, 1)
                P.add("sync", lambda h: h.dma_start(out=Xv[:, :, HALO:TW], in_=x1v[:, :, i, :]),
                      reads=[("x1d", i)], writes=xkeys, dma=True, slot="x")

                def s1(idx):
                    hh, kt = pairs[idx]
                    hb = hh % 2
                    zb = idx % 4
                    Z, zk = PS[zb], ("ps", zb)
                    masked = kt >= 8 * i
                    P.add("pe", lambda h: h.matmul(Z[:], lhsT=KTB[hb][:, kt * 128:(kt + 1) * 128], rhs=QTv[:, hh, :],
                                                   start=True, stop=True),
                          reads=[("ktb", hb), ("q", hh)], writes=[zk])
                    if masked:
                        m = kt - 8 * i
                        P.add("pe", lambda h: h.matmul(Z[:], lhsT=ident, rhs=NEGMv[:, m, :], start=False, stop=True),
                              reads=["negm", "cst", zk], writes=[zk])
                    eb = EB[idx % 2]
                    P.add("act", lambda h: h.activation(out=eb, in_=Z[:], func=AF.Exp),
                          reads=[zk], writes=[("eb", idx % 2)])

                def s1b(idx):
                    eb = EB[idx % 2]
                    sp = SPB[idx % 3]
                    P.add("act", lambda h: h.activation(out=sp, in_=eb, func=AF.Ln, bias=1.0, scale=1.0),
                          reads=[("eb", idx % 2)], writes=[("sp", idx % 3)])

                def s3(idx):
                    hh, kt = pairs[idx]
                    zb = idx % 4
                    Z, zk = PS[zb], ("ps", zb)
                    sp = SPB[idx % 3]
                    first = kt == nk - 1
                    last = kt == 0
                    sprev = SB_[(idx - 1) % 2]
                    scur = SB_[idx % 2]
                    P.add("pe", lambda h: h.matmul(Z[:], lhsT=negU, rhs=sp, start=False, stop=True),
                          reads=[("sp", idx % 3), "cst", zk], writes=[zk])
                    if not first:
                        P.add("pe", lambda h: h.matmul(Z[:], lhsT=negones, rhs=sprev, start=False, stop=True),
                              reads=[("s", (idx - 1) % 2), "cst", zk], writes=[zk])
                    if not last:
                        if first:
                            P.add("pool", lambda h: h.tensor_copy(out=scur, in_=sp),
                                  reads=[("sp", idx % 3)], writes=[("s", idx % 2)])
                        else:
                            P.add("pool", lambda h: h.tensor_tensor(out=scur, in0=sprev, in1=sp, op=ALU.add),
                                  reads=[("sp", idx % 3), ("s", (idx - 1) % 2)], writes=[("s", idx % 2)])
                    ab = AB[idx % 3]
                    P.add("act", lambda h: h.activation(out=ab, in_=Z[:], func=AF.Exp),
                          reads=[zk], writes=[("ab", idx % 3)])

                def s5(idx):
                    hh, kt = pairs[idx]
                    hb = hh % 2
                    ob = 4 + hb
                    O, ok = PS[ob], ("ps", ob)
                    ab = AB[idx % 3]
                    P.add("pe", lambda h: h.matmul(O[:], lhsT=VB[hb][:, kt, :], rhs=ab,
                                                   start=(kt == nk - 1), stop=(kt == 0)),
                          reads=[("vb", hb), ("ab", idx % 3)], writes=[ok])
                    if kt == 0:
                        P.add("dve", lambda h: h.tensor_copy(out=OTv[:, hh, :], in_=O[:]),
                              reads=[ok], writes=[("o", hh)])
                        if hh + 2 < H:
                            kv_load(i, hh + 2)

                for step in range(n + 2):
                    if step < n:
                        s1(step)
                    if 0 <= step - 1 < n:
                        s3(step - 1)
                    if step < n:
                        s1b(step)
                    if 0 <= step - 2 < n:
                        s5(step - 2)
                cnt["bank"] = 6

            for i in range(NT):
                qsrc = qsp_t[i].rearrange("g (h p) c -> p g h c", p=128)
                P.add("sync", lambda h, qsrc=qsrc: h.dma_start(
                    out=QT.rearrange("p (g h c) -> p g h c", g=4, h=4), in_=qsrc),
                      reads=[("qsp", i, g_) for g_ in range(4)], writes=[("q", hh) for hh in range(H)], dma=True, slot="q")
                attention(i)
                def evac_o(dc, pb, pk):
                    P.add("dve", lambda h, pb=pb, dc=dc: h.tensor_tensor(
                        out=Xv[:, dc, HALO:TW], in0=pb[:], in1=Xv[:, dc, HALO:TW], op=ALU.add),
                          reads=[pk, ("x", dc)], writes=[("x", dc)])

                proj_fm(w_o_d, 0, 0, H, 4, lambda hc: OTv[:, hc, :], lambda hc: ("o", hc), evac_o)
                mlp(1, V_M1)
                rms_stats(HALO, T)
                for kc in range(KC):
                    P.add("dve", lambda h, kc=kc: h.scalar_tensor_tensor(
                        out=Xv[:, kc, HALO:TW], in0=Xv[:, kc, HALO:TW], scalar=VECS[:, V_FN + kc:V_FN + kc + 1],
                        in1=RSTD[:, HALO:TW], op0=ALU.mult, op1=ALU.mult),
                          reads=[("x", kc), "rstd", "vecs"], writes=[("x", kc)])
                finals.append(P.add("act", lambda h, i=i: h.dma_start(out=odst[:, :, i, :], in_=Xv[:, :, HALO:TW]),
                                    reads=xkeys, dma=True, slot="out"))


            return finals

        P_real = P
        P = Prog(nc)
        state["dry"] = True
        construct()
        P = P_real
        for k_ in cnt:
            cnt[k_] = 0
        state["dry"] = False
        state["issued"] = 0
        finals = construct()
        P.final_wait("act", finals)
        P.emit()
    return nc


def _vec_layout(v):
    return np.ascontiguousarray(np.asarray(v, np.float32).reshape(KC, 128).T)


def _consts():
    j = np.arange(128)
    ident = np.eye(128, dtype=np.float32)
    negU = -(j[:, None] >= j[None, :]).astype(np.float32)
    negones = -np.ones((128, 128), np.float32)
    meanones = np.full((128, 128), 1.0 / D, np.float32)
    return np.concatenate([ident, negU, negones, meanones], axis=1).astype(ml_dtypes.bfloat16)


_CACHE = {}


def _prog():
    if "p" not in _CACHE:
        _CACHE["p"] = build()
    return _CACHE["p"]


def kernel(x, pool_norm, pool_w, pool_scale, kv_norm, w_kv, attn_norm, w_q, w_o, mlp_norm, w_up, w_down,
           final_norm):
    x = np.asarray(x, np.float32)
    B = x.shape[0]
    n_cores = 8
    vecs = np.concatenate([
        _vec_layout(pool_norm[0]), _vec_layout(pool_scale[0]), _vec_layout(kv_norm), _vec_layout(attn_norm[0]),
        _vec_layout(mlp_norm[0]), _vec_layout(mlp_norm[1]), _vec_layout(final_norm)], axis=1)
    vecs = np.ascontiguousarray(vecs, np.float32)
    cst = _consts()
    w_up = np.ascontiguousarray(w_up, np.float32)
    w_down = np.ascontiguousarray(w_down, np.float32)
    pool_w3 = np.ascontiguousarray(np.asarray(pool_w, np.float32)[0])
    w_kv = np.ascontiguousarray(w_kv, np.float32)
    w_q2 = np.ascontiguousarray(np.asarray(w_q, np.float32)[0])
    w_o2 = np.ascontiguousarray(np.asarray(w_o, np.float32)[0])

    ins = []
    for c in range(n_cores):
        b, p = c // 2, c % 2
        xT = np.zeros((D, NT, TW), np.float32)
        for i in range(NT):
            j = 2 * i + p
            xT[:, i, HALO:] = x[b, j * T:(j + 1) * T, :].T
            if j > 0:
                xT[:, i, :HALO] = x[b, j * T - HALO:j * T, :].T
        pos = p * T + np.arange(T)
        ic = np.stack([1.0 / np.minimum(pos + 1, w) for w in WINS]).astype(np.float32)
        invcnt = np.ascontiguousarray(np.broadcast_to(ic.reshape(1, 4 * T), (128, 4 * T)))
        sk = np.arange(128)[:, None, None]
        m = np.arange(8)[None, :, None]
        tq = np.arange(T)[None, None, :]
        negm = np.where(m * 128 + sk >= p * T + tq, -30000.0, 0.0).astype(np.float32)
        negm = negm.reshape(128, 8 * T).astype(ml_dtypes.bfloat16)
        ins.append({"vecs": vecs, "cst": cst, "w_up": w_up, "w_down": w_down, "xT": xT, "invcnt": invcnt,
                    "pool_w": pool_w3, "w_kv": w_kv, "negm": negm, "w_q": w_q2, "w_o": w_o2})
    res = run_bass_kernel_spmd(_prog(), ins, core_ids=list(range(n_cores))).results

    out = np.empty((B, SEQ, D), np.float32)
    for c in range(n_cores):
        b, p = c // 2, c % 2
        oT = res[c]["outT"]
        for i in range(NT):
            j = 2 * i + p
            out[b, j * T:(j + 1) * T, :] = oT[:, i, :].T
    return out
```

```python
import contextlib
import numpy as np
import ml_dtypes
import concourse.bass as bass
import concourse.mybir as mybir
from concourse.bass_utils import run_bass_kernel_spmd

F32 = mybir.dt.float32
BF16 = mybir.dt.bfloat16
AF = mybir.ActivationFunctionType
ALU = mybir.AluOpType

D = 2048
KC = 16
T = 512
NT = 4
HALO = 16
TW = T + HALO
DFF = 8192
H = 16
SEQ = 4096
EPS = 1e-6
WINS = (2, 4, 8, 16)
NS = 4
NB = 4
LA = NB - 1
SLAB = 2048
INV_SQRT = 1.0 / float(np.sqrt(128.0))
V_PN, V_PS, V_KV, V_AN, V_M0, V_M1, V_FN = [16 * k for k in range(7)]


class Op:
    __slots__ = ("eng", "fn", "deps", "dma", "slot", "ev", "needed", "inc")

    def __init__(self, eng, fn, dma, slot, inc=16):
        self.eng = eng
        self.fn = fn
        self.dma = dma
        self.slot = slot
        self.inc = inc
        self.deps = ()
        self.ev = None
        self.needed = False


class Prog:
    ENGS = ("sync", "act", "dve", "pool", "pe")

    def __init__(self, nc):
        self.nc = nc
        self.ops = {e: [] for e in self.ENGS}
        self.last_w = {}
        self.readers = {}
        self.finals = []
        self.pending = {e: set() for e in self.ENGS}
        self.dma_since = []

    def retire(self, keys):
        ops = set()
        for k in keys:
            w = self.last_w.get(k)
            if w is not None:
                ops.add(w)
            ops.update(self.readers.get(k, ()))
        return ops

    def seed(self, key, ops):
        self.readers.setdefault(key, []).extend(ops)

    def add(self, eng, fn, reads=(), writes=(), dma=False, slot=None, inc=16):
        op = Op(eng, fn, dma, slot, inc)
        deps = set()
        for k in reads:
            w = self.last_w.get(k)
            if w is not None:
                deps.add(w)
        for k in writes:
            w = self.last_w.get(k)
            if w is not None:
                deps.add(w)
            for r in self.readers.get(k, ()):
                deps.add(r)
        if self.pending[eng]:
            deps |= self.pending[eng]
            self.pending[eng] = set()
        if eng == "pe" and not dma:
            deps = {d for d in deps if not (d.eng == "pe" and not d.dma)}
        op.deps = deps
        for d in deps:
            d.needed = True
        for k in reads:
            self.readers.setdefault(k, []).append(op)
        for k in writes:
            self.last_w[k] = op
            self.readers[k] = []
        self.ops[eng].append(op)
        if dma:
            self.dma_since.append(op)
        return op

    def barrier(self):
        deps = set(self.dma_since)
        for e in ("act", "dve", "pool", "pe"):
            for op in reversed(self.ops[e]):
                if not op.dma:
                    deps.add(op)
                    break
        for e in self.ENGS:
            self.pending[e] |= deps
        self.dma_since = []

    def final_wait(self, eng, ops):
        for o in ops:
            o.needed = True
        self.finals.append((eng, list(ops)))

    def emit(self):
        nc = self.nc
        slots = []
        for e in self.ENGS:
            for op in self.ops[e]:
                if op.dma and op.slot not in slots:
                    slots.append(op.slot)
        with contextlib.ExitStack() as st:
            esem = {}
            for e in ("act", "dve", "pool", "pe"):
                esem[e] = st.enter_context(nc.semaphore("s_" + e))
            ssem = {}
            for i, s in enumerate(slots):
                ssem[s] = st.enter_context(nc.semaphore("d_%d" % i))
            cnt = {e: 0 for e in esem}
            scnt = {s: 0 for s in slots}
            for e in self.ENGS:
                for op in self.ops[e]:
                    if op.dma:
                        scnt[op.slot] += op.inc
                        op.ev = (ssem[op.slot], scnt[op.slot])
                    elif op.needed:
                        cnt[e] += 1
                        op.ev = (esem[e], cnt[e])
            block = st.enter_context(nc.Block())
            handles = {
                "sync": block.sync,
                "act": block.scalar,
                "dve": block.vector,
                "pool": block.gpsimd,
                "pe": block.tensor,
            }

            def make(e):
                def body(h):
                    waited = {}

                    def wait_for(d):
                        sem, val = d.ev
                        if waited.get(id(sem), 0) < val:
                            h.wait_ge(sem, val)
                            waited[id(sem)] = val

                    for op in self.ops[e]:
                        for d in op.deps:
                            wait_for(d)
                        ins = op.fn(h)
                        if op.dma:
                            ins.then_inc(op.ev[0], op.inc)
                        elif op.needed:
                            ins.then_inc(op.ev[0], 1)
                    for (fe, fops) in self.finals:
                        if fe == e:
                            for d in fops:
                                wait_for(d)

                return body

            for e in self.ENGS:
                handles[e](make(e))


def build():
    nc = bass.Bass("TRN2", target_bir_lowering=False)
    P = Prog(nc)

    def din(name, shape, dt=F32):
        return nc.dram_tensor(name, list(shape), dt, kind="ExternalInput").ap()

    def dout(name, shape, dt=F32):
        return nc.dram_tensor(name, list(shape), dt, kind="ExternalOutput").ap()

    def dint(name, shape, dt=F32):
        return nc.dram_tensor(name, list(shape), dt)

    vecs_d = din("vecs", [128, 7 * 16])
    cst_d = din("cst", [128, 4 * 128], BF16)
    w_up_d = din("w_up", [2, D, DFF])
    w_down_d = din("w_down", [2, DFF, D])
    xT_d = din("xT", [D, NT, TW])
    invcnt_d = din("invcnt", [128, 4 * T])
    pool_w_d = din("pool_w", [4, 512, 512])
    w_kv_d = din("w_kv", [D, 2 * D])
    negm_d = din("negm", [128, 8 * T], BF16)
    w_q_d = din("w_q", [D, D])
    w_o_d = din("w_o", [D, D])
    outT_d = dout("outT", [D, NT, T])
    x1T_t = dint("x1T", [D, NT, T])
    kmine_t = dint("kmine", [NT, 4, 512, 512], BF16)
    vmine_t = dint("vmine", [NT, 4, 512, 512], BF16)
    kall_t = dint("kall", [NT, 4, 1024, 512], BF16)
    vall_t = dint("vall", [NT, 4, 1024, 512], BF16)
    qsp_t = dint("qsp", [NT, 4, 512, 512], BF16)
    x1T_d = x1T_t.ap()

    st = contextlib.ExitStack()
    with st:
        def sb(name, shape, dt):
            return st.enter_context(nc.sbuf_tensor(name, list(shape), dt))

        X = sb("X", [128, KC * TW], F32)
        XN = sb("XN", [128, KC * T], BF16)
        G = sb("G", [128, 32 * T], BF16)
        RSTD = sb("RSTD", [128, TW], F32)
        SQ = [sb("SQ%d" % k, [128, TW], BF16) for k in range(2)]
        RL = [sb("RL%d" % k, [128, T], BF16) for k in range(2)]
        STG = [sb("STG%d" % k, [128, SLAB], F32) for k in range(NS)]
        WB = [sb("WB%d" % k, [128, SLAB], BF16) for k in range(NB)]
        CST = sb("CST", [128, 4 * 128], BF16)
        VECS = sb("VECS", [128, 7 * 16], F32)
        EPSB = sb("EPSB", [128, 1], F32)
        PHN = 26624
        PH = sb("PH", [128, PHN], BF16)
        PHf = PH[:].bitcast(F32)

        def carve_bf(off, n):
            return PH[:, off:off + n]

        def carve_f32(off, n):
            return PHf[:, off // 2:(off + n) // 2]

        IC = carve_f32(0, 4096)
        TMPv = [carve_f32(4096 + k * 1056, 1056) for k in range(4)]
        KST = [carve_bf(8320 + k * 2048, 2048) for k in range(2)]
        VST = [carve_bf(12416 + k * 2048, 2048) for k in range(2)]
        NEGM = carve_bf(0, 4096)
        QT = carve_bf(4096, 8192)
        OT = carve_bf(12288, 8192)
        EB = [carve_f32(20480 + k * 1024, 1024) for k in range(2)]
        SPB = [carve_bf(22528 + k * 512, 512) for k in range(3)]
        AB = [carve_bf(24064 + k * 512, 512) for k in range(3)]
        SB_ = [carve_bf(25600 + k * 512, 512) for k in range(2)]
        PS = [st.enter_context(nc.psum_tensor("PS%d" % k, [128, T], F32)) for k in range(8)]

        Xv = X[:].rearrange("p (k c) -> p k c", k=KC)
        XNv = XN[:].rearrange("p (k c) -> p k c", k=KC)
        Gv = G[:].rearrange("p (k c) -> p k c", k=32)
        ident = CST[:, 0:128]
        negU = CST[:, 128:256]
        negones = CST[:, 256:384]
        meanones = CST[:, 384:512]

        cnt = {"bank": 0, "slab": 0, "sq": 0, "rl": 0}

        def bank():
            b = cnt["bank"] % 8
            cnt["bank"] += 1
            return PS[b], ("ps", b)

        plan = []
        state = {"dry": True, "issued": 0}

        def issue_slab(n):
            src3, R, W = plan[n]
            s = n % NS
            b = n % NB
            stv = STG[s][:, 0:R * W]
            P.add("sync", lambda h: h.dma_start(out=stv.rearrange("p (r w) -> p r w", r=R), in_=src3),
                  writes=[("st", s)], dma=True, slot=("st", s))
            wbv = WB[b][:, 0:R * W]
            if n % 2 == 0:
                P.add("dve", lambda h: h.tensor_copy(out=wbv, in_=stv), reads=[("st", s)], writes=[("wb", b)])
            else:
                P.add("act", lambda h: h.activation(out=wbv, in_=stv, func=AF.Copy), reads=[("st", s)],
                      writes=[("wb", b)])

        def load_slab(src3, R, W):
            n = cnt["slab"]
            cnt["slab"] += 1
            b = n % NB
            if state["dry"]:
                plan.append((src3, R, W))
            else:
                while state["issued"] <= min(n + LA, len(plan) - 1):
                    issue_slab(state["issued"])
                    state["issued"] += 1
            return WB[b][:, 0:R * W].rearrange("p (r w) -> p r w", r=R), ("wb", b)

        def wslab(w2d, r0, c0, R, W):
            return w2d[r0:r0 + R * 128, c0:c0 + W].rearrange("(r p) w -> p r w", p=128)

        def proj_fm(w2d, row0, col0, kchunks, ngroups, rhs_fn, rhs_key, evac_fn):
            for og in range(ngroups):
                banks = [bank() for _ in range(4)]
                for kq in range(kchunks // 4):
                    wv, wk = load_slab(wslab(w2d, row0 + kq * 512, col0 + og * 512, 4, 512), 4, 512)
                    for kl in range(4):
                        kc = kq * 4 + kl
                        for ol in range(4):
                            pb, pk = banks[ol]
                            P.add("pe", lambda h, wv=wv, pb=pb, kl=kl, ol=ol, kc=kc: h.matmul(
                                pb[:], lhsT=wv[:, kl, ol * 128:(ol + 1) * 128], rhs=rhs_fn(kc),
                                start=(kc == 0), stop=(kc == kchunks - 1)),
                                  reads=[wk, rhs_key(kc)], writes=[pk])
                for ol in range(4):
                    evac_fn(og * 4 + ol, banks[ol][0], banks[ol][1])

        def rms_stats(c0, ncol):
            segs = []
            c = c0
            if c0 < HALO:
                segs.append((c0, HALO))
                c = HALO
            segs.append((c, c0 + ncol))
            banks = [bank() for _ in segs]
            for kc in range(KC):
                j = cnt["sq"] % 2
                cnt["sq"] += 1
                sq = SQ[j]
                P.add("act", lambda h, sq=sq, kc=kc: h.activation(out=sq[:, c0:c0 + ncol], in_=Xv[:, kc, c0:c0 + ncol],
                                                                  func=AF.Square),
                      reads=[("x", kc)], writes=[("sq", j)])
                for (a, b_), (pb, pk) in zip(segs, banks):
                    P.add("pe", lambda h, sq=sq, a=a, b_=b_, pb=pb, kc=kc: h.matmul(
                        pb[:, 0:b_ - a], lhsT=meanones, rhs=sq[:, a:b_], start=(kc == 0), stop=(kc == KC - 1)),
                          reads=[("sq", j), "cst"], writes=[pk])
            for (a, b_), (pb, pk) in zip(segs, banks):
                P.add("act", lambda h, a=a, b_=b_, pb=pb: h.activation(out=RSTD[:, a:b_], in_=pb[:, 0:b_ - a],
                                                                       func=AF.Sqrt, bias=EPSB[:], scale=1.0),
                      reads=[pk, "eps"], writes=["rstd"])
            P.add("dve", lambda h: h.reciprocal(out=RSTD[:, c0:c0 + ncol], in_=RSTD[:, c0:c0 + ncol]),
                  reads=["rstd"], writes=["rstd"])

        def norm_to_xn(vcol):
            rms_stats(HALO, T)
            for kc in range(KC):
                P.add("dve", lambda h, kc=kc: h.scalar_tensor_tensor(
                    out=XNv[:, kc, :], in0=Xv[:, kc, HALO:TW], scalar=VECS[:, vcol + kc:vcol + kc + 1],
                    in1=RSTD[:, HALO:TW], op0=ALU.mult, op1=ALU.mult),
                      reads=[("x", kc), "rstd", "vecs"], writes=[("xn", kc)])

        def mlp(layer, vcol):
            norm_to_xn(vcol)
            wu = w_up_d[layer]
            wd = w_down_d[layer]
            for fh in range(2):
                def evac_h(fl, pb, pk):
                    j = cnt["rl"] % 2
                    cnt["rl"] += 1
                    rl = RL[j]
                    P.add("act", lambda h, rl=rl, pb=pb: h.activation(out=rl[:], in_=pb[:], func=AF.Relu),
                          reads=[pk], writes=[("rl", j)])
                    P.add("dve", lambda h, rl=rl, fl=fl: h.tensor_tensor(out=Gv[:, fl, :], in0=rl[:], in1=rl[:],
                                                                         op=ALU.mult),
                          reads=[("rl", j)], writes=[("g", fl)])

                proj_fm(wu, 0, fh * 4096, KC, 8, lambda kc: XNv[:, kc, :], lambda kc: ("xn", kc), evac_h)

                def evac_y(dc, pb, pk):
                    P.add("dve", lambda h, pb=pb, dc=dc: h.tensor_tensor(
                        out=Xv[:, dc, HALO:TW], in0=pb[:], in1=Xv[:, dc, HALO:TW], op=ALU.add),
                          reads=[pk, ("x", dc)], writes=[("x", dc)])

                proj_fm(wd, fh * 4096, 0, 32, 4, lambda fc: Gv[:, fc, :], lambda fc: ("g", fc), evac_y)

        def construct():
            finals = []
            P.add("sync", lambda h: h.dma_start(out=CST[:], in_=cst_d), writes=["cst"], dma=True, slot="cst")
            P.add("sync", lambda h: h.dma_start(out=VECS[:], in_=vecs_d), writes=["vecs"], dma=True, slot="vecs")
            P.add("dve", lambda h: h.memset(EPSB[:], EPS), writes=["eps"])
            xkeys = [("x", kc) for kc in range(KC)]

            P.add("sync", lambda h: h.dma_start(out=IC, in_=invcnt_d), writes=["ic"], dma=True, slot="ic")
            ICv = IC.rearrange("p (g c) -> p g c", g=4)
            xsrc = xT_d.rearrange("(k p) i c -> p k i c", p=128)
            x1v = x1T_d.rearrange("(k p) i c -> p k i c", p=128)
            GROUPS = [[0, 1], [2, 3], [4, 5], [6, 7]]
            for i in range(NT):
                P.add("sync", lambda h, i=i: h.dma_start(out=Xv, in_=xsrc[:, :, i, :]),
                      writes=xkeys, dma=True, slot="x")
                rms_stats(0, TW)
                for kc in range(KC):
                    g = kc // 4
                    w = WINS[g]
                    e = "dve"
                    xnp = TMPv[0]
                    P.add(e, lambda h, kc=kc, xnp=xnp: h.scalar_tensor_tensor(
                        out=xnp, in0=Xv[:, kc, :], scalar=VECS[:, V_PN + kc:V_PN + kc + 1], in1=RSTD[:],
                        op0=ALU.mult, op1=ALU.mult),
                          reads=[("x", kc), "rstd", "vecs"], writes=[("tmp", 0)])
                    cur, curk = xnp, ("tmp", 0)
                    m = 2
                    lvl = 0
                    while m <= w:
                        dst = TMPv[1 + lvl % 2]
                        dk = ("tmp", 1 + lvl % 2)
                        hm = m // 2
                        P.add(e, lambda h, dst=dst, cur=cur, m=m, hm=hm: h.tensor_tensor(
                            out=dst[:, m - 1:TW], in0=cur[:, m - 1:TW], in1=cur[:, m - 1 - hm:TW - hm], op=ALU.add),
                              reads=[curk], writes=[dk])
                        cur, curk = dst, dk
                        m *= 2
                        lvl += 1
                    if i == 0:
                        t3 = TMPv[3]
                        P.add(e, lambda h, t3=t3, cur=cur, g=g: h.tensor_tensor(
                            out=t3[:, 0:T], in0=cur[:, HALO:TW], in1=ICv[:, g, :], op=ALU.mult),
                              reads=[curk, "ic"], writes=[("tmp", 3)])
                        P.add(e, lambda h, t3=t3, xnp=xnp, kc=kc: h.tensor_tensor(
                            out=XNv[:, kc, :], in0=t3[:, 0:T], in1=xnp[:, HALO:TW], op=ALU.subtract),
                              reads=[("tmp", 3), ("tmp", 0)], writes=[("xn", kc)])
                    else:
                        P.add(e, lambda h, cur=cur, xnp=xnp, kc=kc, w=w: h.scalar_tensor_tensor(
                            out=XNv[:, kc, :], in0=cur[:, HALO:TW], scalar=1.0 / w, in1=xnp[:, HALO:TW],
                            op0=ALU.mult, op1=ALU.subtract),
                              reads=[curk, ("tmp", 0)], writes=[("xn", kc)])
                for g in range(4):
                    def evac_p(ec, pb, pk, g=g):
                        dc = 4 * g + ec
                        P.add("dve", lambda h, pb=pb, dc=dc: h.scalar_tensor_tensor(
                            out=Xv[:, dc, HALO:TW], in0=pb[:], scalar=VECS[:, V_PS + dc:V_PS + dc + 1],
                            in1=Xv[:, dc, HALO:TW], op0=ALU.mult, op1=ALU.add),
                              reads=[pk, ("x", dc), "vecs"], writes=[("x", dc)])

                    proj_fm(pool_w_d[g], 0, 0, 4, 1, lambda cc, g=g: XNv[:, 4 * g + cc, :],
                            lambda cc, g=g: ("xn", 4 * g + cc), evac_p)
                mlp(0, V_M0)
                P.add("act", lambda h, i=i: h.dma_start(out=x1v[:, :, i, :], in_=Xv[:, :, HALO:TW]),
                      reads=xkeys, writes=[("x1d", i)], dma=True, slot="x1")
                norm_to_xn(V_KV)
                for hg in range(4):
                    kst = KST[hg % 2]
                    kstv = kst.rearrange("p (h c) -> p h c", h=4)

                    def evac_k(hh, pb, pk, kstv=kstv, hg=hg):
                        hl = hh % 4
                        P.add("act", lambda h, pb=pb, hl=hl: h.activation(out=kstv[:, hl, :], in_=pb[:], func=AF.Copy),
                              reads=[pk], writes=[("kst", hg % 2)])

                    def proj_k(hg=hg, evac_k=evac_k):
                        banks = [bank() for _ in range(4)]
                        for kq in range(4):
                            wv, wk = load_slab(wslab(w_kv_d, kq * 512, hg * 512, 4, 512), 4, 512)
                            for kl in range(4):
                                kc = kq * 4 + kl
                                for ol in range(4):
                                    pb, pk = banks[ol]
                                    P.add("pe", lambda h, wv=wv, pb=pb, kl=kl, ol=ol, kc=kc: h.matmul(
                                        pb[:], lhsT=wv[:, kl, ol * 128:(ol + 1) * 128], rhs=XNv[:, kc, :],
                                        start=(kc == 0), stop=(kc == KC - 1)),
                                          reads=[wk, ("xn", kc)], writes=[pk])
                        for ol in range(4):
                            evac_k(hg * 4 + ol, banks[ol][0], banks[ol][1])

                    proj_k()
                    dst = kmine_t[i, hg].rearrange("(h p) c -> p h c", p=128)
                    P.add("act", lambda h, dst=dst, kstv=kstv: h.dma_start(out=dst, in_=kstv),
                          reads=[("kst", hg % 2)], writes=[("kmine", i, hg)], dma=True, slot=("kst", hg % 2))
                    P.add("pool", lambda h, i=i, hg=hg: h.collective_compute(
                        "AllGather", ALU.bypass, replica_groups=GROUPS, ins=[kmine_t[i, hg]], outs=[kall_t[i, hg]]),
                          reads=[("kmine", i, hg)], writes=[("kall", i, hg)], dma=True, slot="cc", inc=1)
                for hg in range(4):
                    vst = VST[hg % 2]
                    vstv = vst.rearrange("p (h t d) -> p h t d", h=4, t=4)
                    banks = [bank() for _ in range(4)]
                    for kq in range(4):
                        wv, wk = load_slab(wslab(w_kv_d, kq * 512, D + hg * 512, 4, 512), 4, 512)
                        for kl in range(4):
                            kc = kq * 4 + kl
                            for tt in range(4):
                                pb, pk = banks[tt]
                                P.add("pe", lambda h, wv=wv, pb=pb, kl=kl, kc=kc, tt=tt: h.matmul(
                                    pb[:], lhsT=XNv[:, kc, tt * 128:(tt + 1) * 128], rhs=wv[:, kl, :],
                                    start=(kc == 0), stop=(kc == KC - 1)),
                                      reads=[wk, ("xn", kc)], writes=[pk])
                    for tt in range(4):
                        pb, pk = banks[tt]
                        P.add("dve", lambda h, pb=pb, vstv=vstv, tt=tt: h.tensor_copy(
                            out=vstv[:, :, tt, :], in_=pb[:].rearrange("p (h d) -> p h d", h=4)),
                              reads=[pk], writes=[("vst", hg % 2)])
                    dst = vmine_t[i, hg].rearrange("(h p) (t d) -> p h t d", p=128, t=4)
                    P.add("act", lambda h, dst=dst, vstv=vstv: h.dma_start(out=dst, in_=vstv),
                          reads=[("vst", hg % 2)], writes=[("vmine", i, hg)], dma=True, slot=("vst", hg % 2))
                    P.add("pool", lambda h, i=i, hg=hg: h.collective_compute(
                        "AllGather", ALU.bypass, replica_groups=GROUPS, ins=[vmine_t[i, hg]], outs=[vall_t[i, hg]]),
                          reads=[("vmine", i, hg)], writes=[("vall", i, hg)], dma=True, slot="cc", inc=1)

                norm_to_xn(V_AN)
                for hg in range(4):
                    kst = KST[hg % 2]
                    kstv = kst.rearrange("p (h c) -> p h c", h=4)
                    banks = [bank() for _ in range(4)]
                    for kq in range(4):
                        wv, wk = load_slab(wslab(w_q_d, kq * 512, hg * 512, 4, 512), 4, 512)
                        for kl in range(4):
                            kc = kq * 4 + kl
                            for ol in range(4):
                                pb, pk = banks[ol]
                                P.add("pe", lambda h, wv=wv, pb=pb, kl=kl, ol=ol, kc=kc: h.matmul(
                                    pb[:], lhsT=wv[:, kl, ol * 128:(ol + 1) * 128], rhs=XNv[:, kc, :],
                                    start=(kc == 0), stop=(kc == KC - 1)),
                                      reads=[wk, ("xn", kc)], writes=[pk])
                    for ol in range(4):
                        pb, pk = banks[ol]
                        P.add("act", lambda h, pb=pb, ol=ol, kstv=kstv: h.activation(
                            out=kstv[:, ol, :], in_=pb[:], func=AF.Copy, scale=INV_SQRT),
                              reads=[pk], writes=[("kst", hg % 2)])
                    dst = qsp_t[i, hg].rearrange("(h p) c -> p h c", p=128)
                    P.add("act", lambda h, dst=dst, kstv=kstv: h.dma_start(out=dst, in_=kstv),
                          reads=[("kst", hg % 2)], writes=[("qsp", i, hg)], dma=True, slot=("kst", hg % 2))

            a_ops = P.retire(["ic", ("tmp", 0), ("tmp", 1), ("tmp", 2), ("tmp", 3), ("kst", 0), ("kst", 1),
                              ("vst", 0), ("vst", 1)])
            bkeys = (["negm"] + [("q", h_) for h_ in range(H)] + [("o", h_) for h_ in range(H)]
                     + [("eb", k) for k in range(2)] + [("sp", k) for k in range(3)] + [("ab", k) for k in range(3)]
                     + [("s", k) for k in range(2)])
            for k in bkeys:
                P.seed(k, a_ops)

            P.add("sync", lambda h: h.dma_start(out=NEGM, in_=negm_d), writes=["negm"], dma=True, slot="negm")
            NEGMv = NEGM.rearrange("p (m c) -> p m c", m=8)
            QTv = QT.rearrange("p (h c) -> p h c", h=H)
            OTv = OT.rearrange("p (h c) -> p h c", h=H)
            odst = outT_d.rearrange("(k p) i c -> p k i c", p=128)
            KTB = [G[:, hb * 8192:hb * 8192 + 4096] for hb in range(2)]
            VB = [G[:, hb * 8192 + 4096:hb * 8192 + 8192].rearrange("p (t d) -> p t d", d=128) for hb in range(2)]
            gkeys = [("g", k) for k in range(32)]

            def kv_load(i, hh):
                hb = hh % 2
                hg, hl = hh // 4, hh % 4
                nch = i + 1
                for r in range(2):
                    r0 = r * 512 + hl * 128
                    ktd = KTB[hb].rearrange("p (c r t) -> p c r t", r=2, t=T)[:, 0:nch, r, :]
                    kts = kall_t[0:nch, hg, r0:r0 + 128, :].rearrange("i p c -> p i c")
                    P.add("sync", lambda h, ktd=ktd, kts=kts: h.dma_start(out=ktd, in_=kts),
                          reads=[("kall", i2, hg) for i2 in range(nch)],
                          writes=[("ktb", hb)] + gkeys[hb * 16:hb * 16 + 8], dma=True, slot=("ktb", hb))
                    vd = VB[hb].rearrange("p (c r t) d -> p c r t d", r=2, t=4)[:, 0:nch, r, :, :]
                    vs = vall_t[0:nch, hg, r0:r0 + 128, :].rearrange("i p (t d) -> p i t d", t=4)
                    P.add("sync", lambda h, vd=vd, vs=vs: h.dma_start(out=vd, in_=vs),
                          reads=[("vall", i2, hg) for i2 in range(nch)],
                          writes=[("vb", hb)] + gkeys[hb * 16 + 8:hb * 16 + 16], dma=True, slot=("vb", hb))

            def attention(i):
                nk = 8 * i + 8
                pairs = [(hh, kt) for hh in range(H) for kt in range(nk - 1, -1, -1)]
                n = len(pairs)
                kv_load(i, 0)
                kv_load(i, 1)
                P.add("sync", lambda h: h.dma_start(out=Xv[:, :, HALO:TW], in_=x1v[:, :, i, :]),
                      reads=[("x1d", i)], writes=xkeys, dma=True, slot="x")

                def s1(idx):
                    hh, kt = pairs[idx]
                    hb = hh % 2
                    zb = idx % 4
                    Z, zk = PS[zb], ("ps", zb)
                    masked = kt >= 8 * i
                    P.add("pe", lambda h: h.matmul(Z[:], lhsT=KTB[hb][:, kt * 128:(kt + 1) * 128], rhs=QTv[:, hh, :],
                                                   start=True, stop=True),
                          reads=[("ktb", hb), ("q", hh)], writes=[zk])
                    if masked:
                        m = kt - 8 * i
                        P.add("pe", lambda h: h.matmul(Z[:], lhsT=ident, rhs=NEGMv[:, m, :], start=False, stop=True),
                              reads=["negm", "cst", zk], writes=[zk])
                    eb = EB[idx % 2]
                    P.add("act", lambda h: h.activation(out=eb, in_=Z[:], func=AF.Exp),
                          reads=[zk], writes=[("eb", idx % 2)])

                def s1b(idx):
                    eb = EB[idx % 2]
                    sp = SPB[idx % 3]
                    P.add("act", lambda h: h.activation(out=sp, in_=eb, func=AF.Ln, bias=1.0, scale=1.0),
                          reads=[("eb", idx % 2)], writes=[("sp", idx % 3)])

                def s3(idx):
                    hh, kt = pairs[idx]
                    zb = idx % 4
                    Z, zk = PS[zb], ("ps", zb)
                    sp = SPB[idx % 3]
                    first = kt == nk - 1
                    last = kt == 0
                    sprev = SB_[(idx - 1) % 2]
                    scur = SB_[idx % 2]
                    P.add("pe", lambda h: h.matmul(Z[:], lhsT=negU, rhs=sp, start=False, stop=True),
                          reads=[("sp", idx % 3), "cst", zk], writes=[zk])
                    if not first:
                        P.add("pe", lambda h: h.matmul(Z[:], lhsT=negones, rhs=sprev, start=False, stop=True),
                              reads=[("s", (idx - 1) % 2), "cst", zk], writes=[zk])
                    if not last:
                        if first:
                            P.add("pool", lambda h: h.tensor_copy(out=scur, in_=sp),
                                  reads=[("sp", idx % 3)], writes=[("s", idx % 2)])
                        else:
                            P.add("pool", lambda h: h.tensor_tensor(out=scur, in0=sprev, in1=sp, op=ALU.add),
                                  reads=[("sp", idx % 3), ("s", (idx - 1) % 2)], writes=[("s", idx % 2)])

                def s4(idx):
                    zb = idx % 4
                    Z, zk = PS[zb], ("ps", zb)
                    ab = AB[idx % 3]
                    P.add("act", lambda h: h.activation(out=ab, in_=Z[:], func=AF.Exp),
                          reads=[zk], writes=[("ab", idx % 3)])

                def s5(idx):
                    hh, kt = pairs[idx]
                    hb = hh % 2
                    ob = 4 + hb
                    O, ok = PS[ob], ("ps", ob)
                    ab = AB[idx % 3]
                    P.add("pe", lambda h: h.matmul(O[:], lhsT=VB[hb][:, kt, :], rhs=ab,
                                                   start=(kt == nk - 1), stop=(kt == 0)),
                          reads=[("vb", hb), ("ab", idx % 3)], writes=[ok])
                    if kt == 0:
                        P.add("dve", lambda h: h.tensor_copy(out=OTv[:, hh, :], in_=O[:]),
                              reads=[ok], writes=[("o", hh)])
                        if hh + 2 < H:
                            kv_load(i, hh + 2)

                for step in range(n + 3):
                    if step < n:
                        s1(step)
                    if 0 <= step - 2 < n:
                        s4(step - 2)
                    if step < n:
                        s1b(step)
                    if 0 <= step - 1 < n:
                        s3(step - 1)
                    if 0 <= step - 3 < n:
                        s5(step - 3)
                cnt["bank"] = 6

            for i in range(NT):
                qsrc = qsp_t[i].rearrange("g (h p) c -> p g h c", p=128)
                P.add("sync", lambda h, qsrc=qsrc: h.dma_start(
                    out=QT.rearrange("p (g h c) -> p g h c", g=4, h=4), in_=qsrc),
                      reads=[("qsp", i, g_) for g_ in range(4)], writes=[("q", hh) for hh in range(H)], dma=True, slot="q")
                attention(i)
                def evac_o(dc, pb, pk):
                    P.add("dve", lambda h, pb=pb, dc=dc: h.tensor_tensor(
                        out=Xv[:, dc, HALO:TW], in0=pb[:], in1=Xv[:, dc, HALO:TW], op=ALU.add),
                          reads=[pk, ("x", dc)], writes=[("x", dc)])

                proj_fm(w_o_d, 0, 0, H, 4, lambda hc: OTv[:, hc, :], lambda hc: ("o", hc), evac_o)
                mlp(1, V_M1)
                rms_stats(HALO, T)
                for kc in range(KC):
                    P.add("dve", lambda h, kc=kc: h.scalar_tensor_tensor(
                        out=Xv[:, kc, HALO:TW], in0=Xv[:, kc, HALO:TW], scalar=VECS[:, V_FN + kc:V_FN + kc + 1],
                        in1=RSTD[:, HALO:TW], op0=ALU.mult, op1=ALU.mult),
                          reads=[("x", kc), "rstd", "vecs"], writes=[("x", kc)])
                finals.append(P.add("act", lambda h, i=i: h.dma_start(out=odst[:, :, i, :], in_=Xv[:, :, HALO:TW]),
                                    reads=xkeys, dma=True, slot="out"))


            return finals

        P_real = P
        P = Prog(nc)
        state["dry"] = True
        construct()
        P = P_real
        for k_ in cnt:
            cnt[k_] = 0
        state["dry"] = False
        state["issued"] = 0
        finals = construct()
        P.final_wait("act", finals)
        P.emit()
    return nc


def _vec_layout(v):
    return np.ascontiguousarray(np.asarray(v, np.float32).reshape(KC, 128).T)


def _consts():
    j = np.arange(128)
    ident = np.eye(128, dtype=np.float32)
    negU = -(j[:, None] >= j[None, :]).astype(np.float32)
    negones = -np.ones((128, 128), np.float32)
    meanones = np.full((128, 128), 1.0 / D, np.float32)
    return np.concatenate([ident, negU, negones, meanones], axis=1).astype(ml_dtypes.bfloat16)


_CACHE = {}


def _prog():
    if "p" not in _CACHE:
        _CACHE["p"] = build()
    return _CACHE["p"]


def kernel(x, pool_norm, pool_w, pool_scale, kv_norm, w_kv, attn_norm, w_q, w_o, mlp_norm, w_up, w_down,
           final_norm):
    x = np.asarray(x, np.float32)
    B = x.shape[0]
    n_cores = 8
    vecs = np.concatenate([
        _vec_layout(pool_norm[0]), _vec_layout(pool_scale[0]), _vec_layout(kv_norm), _vec_layout(attn_norm[0]),
        _vec_layout(mlp_norm[0]), _vec_layout(mlp_norm[1]), _vec_layout(final_norm)], axis=1)
    vecs = np.ascontiguousarray(vecs, np.float32)
    cst = _consts()
    w_up = np.ascontiguousarray(w_up, np.float32)
    w_down = np.ascontiguousarray(w_down, np.float32)
    pool_w3 = np.ascontiguousarray(np.asarray(pool_w, np.float32)[0])
    w_kv = np.ascontiguousarray(w_kv, np.float32)
    w_q2 = np.ascontiguousarray(np.asarray(w_q, np.float32)[0])
    w_o2 = np.ascontiguousarray(np.asarray(w_o, np.float32)[0])

    ins = []
    for c in range(n_cores):
        b, p = c // 2, c % 2
        xT = np.zeros((D, NT, TW), np.float32)
        for i in range(NT):
            j = 2 * i + p
            xT[:, i, HALO:] = x[b, j * T:(j + 1) * T, :].T
            if j > 0:
                xT[:, i, :HALO] = x[b, j * T - HALO:j * T, :].T
        pos = p * T + np.arange(T)
        ic = np.stack([1.0 / np.minimum(pos + 1, w) for w in WINS]).astype(np.float32)
        invcnt = np.ascontiguousarray(np.broadcast_to(ic.reshape(1, 4 * T), (128, 4 * T)))
        sk = np.arange(128)[:, None, None]
        m = np.arange(8)[None, :, None]
        tq = np.arange(T)[None, None, :]
        negm = np.where(m * 128 + sk >= p * T + tq, -30000.0, 0.0).astype(np.float32)
        negm = negm.reshape(128, 8 * T).astype(ml_dtypes.bfloat16)
        ins.append({"vecs": vecs, "cst": cst, "w_up": w_up, "w_down": w_down, "xT": xT, "invcnt": invcnt,
                    "pool_w": pool_w3, "w_kv": w_kv, "negm": negm, "w_q": w_q2, "w_o": w_o2})
    res = run_bass_kernel_spmd(_prog(), ins, core_ids=list(range(n_cores))).results

    out = np.empty((B, SEQ, D), np.float32)
    for c in range(n_cores):
        b, p = c // 2, c % 2
        oT = res[c]["outT"]
        for i in range(NT):
            j = 2 * i + p
            out[b, j * T:(j + 1) * T, :] = oT[:, i, :].T
    return out
```

```python
import contextlib
import numpy as np
import ml_dtypes
import concourse.bass as bass
import concourse.mybir as mybir
from concourse.bass_utils import run_bass_kernel_spmd

F32 = mybir.dt.float32
BF16 = mybir.dt.bfloat16
AF = mybir.ActivationFunctionType
ALU = mybir.AluOpType

D = 2048
KC = 16
T = 512
NT = 4
HALO = 16
TW = T + HALO
DFF = 8192
H = 16
SEQ = 4096
EPS = 1e-6
WINS = (2, 4, 8, 16)
NS = 3
NB = 3
LA = NB - 1
SLAB = 2048
INV_SQRT = 1.0 / float(np.sqrt(128.0))
V_PN, V_PS, V_KV, V_AN, V_M0, V_M1, V_FN = [16 * k for k in range(7)]


class Op:
    __slots__ = ("eng", "fn", "deps", "dma", "slot", "ev", "needed", "inc")

    def __init__(self, eng, fn, dma, slot, inc=16):
        self.eng = eng
        self.fn = fn
        self.dma = dma
        self.slot = slot
        self.inc = inc
        self.deps = ()
        self.ev = None
        self.needed = False


class Prog:
    ENGS = ("sync", "act", "dve", "pool", "pe")

    def __init__(self, nc):
        self.nc = nc
        self.ops = {e: [] for e in self.ENGS}
        self.last_w = {}
        self.readers = {}
        self.finals = []
        self.pending = {e: set() for e in self.ENGS}
        self.dma_since = []

    def retire(self, keys):
        ops = set()
        for k in keys:
            w = self.last_w.get(k)
            if w is not None:
                ops.add(w)
            ops.update(self.readers.get(k, ()))
        return ops

    def seed(self, key, ops):
        self.readers.setdefault(key, []).extend(ops)

    def add(self, eng, fn, reads=(), writes=(), dma=False, slot=None, inc=16):
        op = Op(eng, fn, dma, slot, inc)
        deps = set()
        for k in reads:
            w = self.last_w.get(k)
            if w is not None:
                deps.add(w)
        for k in writes:
            w = self.last_w.get(k)
            if w is not None:
                deps.add(w)
            for r in self.readers.get(k, ()):
                deps.add(r)
        if self.pending[eng]:
            deps |= self.pending[eng]
            self.pending[eng] = set()
        if eng == "pe" and not dma:
            deps = {d for d in deps if not (d.eng == "pe" and not d.dma)}
        op.deps = deps
        for d in deps:
            d.needed = True
        for k in reads:
            self.readers.setdefault(k, []).append(op)
        for k in writes:
            self.last_w[k] = op
            self.readers[k] = []
        self.ops[eng].append(op)
        if dma:
            self.dma_since.append(op)
        return op

    def barrier(self):
        deps = set(self.dma_since)
        for e in ("act", "dve", "pool", "pe"):
            for op in reversed(self.ops[e]):
                if not op.dma:
                    deps.add(op)
                    break
        for e in self.ENGS:
            self.pending[e] |= deps
        self.dma_since = []

    def final_wait(self, eng, ops):
        for o in ops:
            o.needed = True
        self.finals.append((eng, list(ops)))

    def emit(self):
        nc = self.nc
        slots = []
        for e in self.ENGS:
            for op in self.ops[e]:
                if op.dma and op.slot not in slots:
                    slots.append(op.slot)
        with contextlib.ExitStack() as st:
            esem = {}
            for e in ("act", "dve", "pool", "pe"):
                esem[e] = st.enter_context(nc.semaphore("s_" + e))
            ssem = {}
            for i, s in enumerate(slots):
                ssem[s] = st.enter_context(nc.semaphore("d_%d" % i))
            cnt = {e: 0 for e in esem}
            scnt = {s: 0 for s in slots}
            for e in self.ENGS:
                for op in self.ops[e]:
                    if op.dma:
                        scnt[op.slot] += op.inc
                        op.ev = (ssem[op.slot], scnt[op.slot])
                    elif op.needed:
                        cnt[e] += 1
                        op.ev = (esem[e], cnt[e])
            block = st.enter_context(nc.Block())
            handles = {
                "sync": block.sync,
                "act": block.scalar,
                "dve": block.vector,
                "pool": block.gpsimd,
                "pe": block.tensor,
            }

            def make(e):
                def body(h):
                    waited = {}

                    def wait_for(d):
                        sem, val = d.ev
                        if waited.get(id(sem), 0) < val:
                            h.wait_ge(sem, val)
                            waited[id(sem)] = val

                    for op in self.ops[e]:
                        for d in op.deps:
                            wait_for(d)
                        ins = op.fn(h)
                        if op.dma:
                            ins.then_inc(op.ev[0], op.inc)
                        elif op.needed:
                            ins.then_inc(op.ev[0], 1)
                    for (fe, fops) in self.finals:
                        if fe == e:
                            for d in fops:
                                wait_for(d)

                return body

            for e in self.ENGS:
                handles[e](make(e))


def build():
    nc = bass.Bass("TRN2", target_bir_lowering=False)
    P = Prog(nc)

    def din(name, shape, dt=F32):
        return nc.dram_tensor(name, list(shape), dt, kind="ExternalInput").ap()

    def dout(name, shape, dt=F32):
        return nc.dram_tensor(name, list(shape), dt, kind="ExternalOutput").ap()

    def dint(name, shape, dt=F32):
        return nc.dram_tensor(name, list(shape), dt)

    vecs_d = din("vecs", [128, 7 * 16])
    cst_d = din("cst", [128, 4 * 128], BF16)
    w_up_d = din("w_up", [2, D, DFF])
    w_down_d = din("w_down", [2, DFF, D])
    xT_d = din("xT", [D, NT, TW])
    invcnt_d = din("invcnt", [128, 4 * T])
    pool_w_d = din("pool_w", [4, 512, 512])
    w_kv_d = din("w_kv", [D, 2 * D])
    negm_d = din("negm", [128, 8 * T], BF16)
    w_q_d = din("w_q", [D, D])
    w_o_d = din("w_o", [D, D])
    outT_d = dout("outT", [D, NT, T])
    x1T_t = dint("x1T", [D, NT, T])
    kmine_t = dint("kmine", [NT, 4, 512, 512], BF16)
    vmine_t = dint("vmine", [NT, 4, 512, 512], BF16)
    kall_t = dint("kall", [NT, 4, 1024, 512], BF16)
    vall_t = dint("vall", [NT, 4, 1024, 512], BF16)
    qsp_t = dint("qsp", [NT, 4, 512, 512], BF16)
    x1T_d = x1T_t.ap()

    st = contextlib.ExitStack()
    with st:
        def sb(name, shape, dt):
            return st.enter_context(nc.sbuf_tensor(name, list(shape), dt))

        X = sb("X", [128, KC * TW], F32)
        XN = sb("XN", [128, KC * T], BF16)
        G = sb("G", [128, 32 * T], BF16)
        RSTD = sb("RSTD", [128, TW], F32)
        SQ = [sb("SQ%d" % k, [128, TW], BF16) for k in range(2)]
        RL = [sb("RL%d" % k, [128, T], BF16) for k in range(2)]
        STG = [sb("STG%d" % k, [128, SLAB], F32) for k in range(NS)]
        WB = [sb("WB%d" % k, [128, SLAB], BF16) for k in range(NB)]
        CST = sb("CST", [128, 4 * 128], BF16)
        VECS = sb("VECS", [128, 7 * 16], F32)
        EPSB = sb("EPSB", [128, 1], F32)
        PHN = 32768
        PH = sb("PH", [128, PHN], BF16)
        PHf = PH[:].bitcast(F32)

        def carve_bf(off, n):
            return PH[:, off:off + n]

        def carve_f32(off, n):
            return PHf[:, off // 2:(off + n) // 2]

        IC = carve_f32(0, 4096)
        TMPv = [carve_f32(4096 + k * 1056, 1056) for k in range(4)]
        KST = [carve_bf(8320 + k * 2048, 2048) for k in range(2)]
        VST = [carve_bf(12416 + k * 2048, 2048) for k in range(2)]
        NEGM = carve_bf(0, 4096)
        QT = carve_bf(4096, 8192)
        OT = carve_bf(12288, 8192)
        EB = [carve_f32(20480, 1024)]
        SPB = [carve_bf(21504 + k * 512, 512) for k in range(2)]
        AB = [carve_bf(22528 + k * 512, 512) for k in range(2)]
        SB_ = [carve_bf(23552 + k * 512, 512) for k in range(2)]
        KTB1 = carve_bf(24576, 4096)
        VB1 = carve_bf(28672, 4096)
        PS = [st.enter_context(nc.psum_tensor("PS%d" % k, [128, T], F32)) for k in range(8)]

        Xv = X[:].rearrange("p (k c) -> p k c", k=KC)
        XNv = XN[:].rearrange("p (k c) -> p k c", k=KC)
        Gv = G[:].rearrange("p (k c) -> p k c", k=32)
        ident = CST[:, 0:128]
        negU = CST[:, 128:256]
        negones = CST[:, 256:384]
        meanones = CST[:, 384:512]

        cnt = {"bank": 0, "slab": 0, "sq": 0, "rl": 0}

        def bank():
            pool_ = state["banks"]
            b = pool_[cnt["bank"] % len(pool_)]
            cnt["bank"] += 1
            return PS[b], ("ps", b)

        plan = []
        state = {"dry": True, "issued": 0, "banks": list(range(8)), "cast": "alt"}

        def issue_slab(n):
            src3, R, W = plan[n]
            s = n % NS
            b = n % NB
            stv = STG[s][:, 0:R * W]
            P.add("sync", lambda h: h.dma_start(out=stv.rearrange("p (r w) -> p r w", r=R), in_=src3),
                  writes=[("st", s)], dma=True, slot=("st", s))
            wbv = WB[b][:, 0:R * W]
            cm = state["cast"]
            if cm == "dve" or (cm == "alt" and n % 2 == 0):
                P.add("dve", lambda h: h.tensor_copy(out=wbv, in_=stv), reads=[("st", s)], writes=[("wb", b)])
            else:
                P.add("act", lambda h: h.activation(out=wbv, in_=stv, func=AF.Copy), reads=[("st", s)],
                      writes=[("wb", b)])

        def load_slab(src3, R, W):
            n = cnt["slab"]
            cnt["slab"] += 1
            b = n % NB
            if state["dry"]:
                plan.append((src3, R, W))
            else:
                while state["issued"] <= min(n + LA, len(plan) - 1):
                    issue_slab(state["issued"])
                    state["issued"] += 1
            return WB[b][:, 0:R * W].rearrange("p (r w) -> p r w", r=R), ("wb", b)

        def wslab(w2d, r0, c0, R, W):
            return w2d[r0:r0 + R * 128, c0:c0 + W].rearrange("(r p) w -> p r w", p=128)

        def proj_fm(w2d, row0, col0, kchunks, ngroups, rhs_fn, rhs_key, evac_fn):
            for og in range(ngroups):
                banks = [bank() for _ in range(4)]
                for kq in range(kchunks // 4):
                    wv, wk = load_slab(wslab(w2d, row0 + kq * 512, col0 + og * 512, 4, 512), 4, 512)
                    for kl in range(4):
                        kc = kq * 4 + kl
                        for ol in range(4):
                            pb, pk = banks[ol]
                            P.add("pe", lambda h, wv=wv, pb=pb, kl=kl, ol=ol, kc=kc: h.matmul(
                                pb[:], lhsT=wv[:, kl, ol * 128:(ol + 1) * 128], rhs=rhs_fn(kc),
                                start=(kc == 0), stop=(kc == kchunks - 1)),
                                  reads=[wk, rhs_key(kc)], writes=[pk])
                    if kq == kchunks // 4 - 1:
                        for ol in range(4):
                            evac_fn(og * 4 + ol, banks[ol][0], banks[ol][1])
                    yield

        def run(gen):
            for _ in gen:
                pass

        def rms_stats(c0, ncol):
            segs = []
            c = c0
            if c0 < HALO:
                segs.append((c0, HALO))
                c = HALO
            segs.append((c, c0 + ncol))
            banks = [bank() for _ in segs]
            for kc in range(KC):
                j = cnt["sq"] % 2
                cnt["sq"] += 1
                sq = SQ[j]
                P.add("act", lambda h, sq=sq, kc=kc: h.activation(out=sq[:, c0:c0 + ncol], in_=Xv[:, kc, c0:c0 + ncol],
                                                                  func=AF.Square),
                      reads=[("x", kc)], writes=[("sq", j)])
                for (a, b_), (pb, pk) in zip(segs, banks):
                    P.add("pe", lambda h, sq=sq, a=a, b_=b_, pb=pb, kc=kc: h.matmul(
                        pb[:, 0:b_ - a], lhsT=meanones, rhs=sq[:, a:b_], start=(kc == 0), stop=(kc == KC - 1)),
                          reads=[("sq", j), "cst"], writes=[pk])
            for (a, b_), (pb, pk) in zip(segs, banks):
                P.add("act", lambda h, a=a, b_=b_, pb=pb: h.activation(out=RSTD[:, a:b_], in_=pb[:, 0:b_ - a],
                                                                       func=AF.Sqrt, bias=EPSB[:], scale=1.0),
                      reads=[pk, "eps"], writes=["rstd"])
            P.add("dve", lambda h: h.reciprocal(out=RSTD[:, c0:c0 + ncol], in_=RSTD[:, c0:c0 + ncol]),
                  reads=["rstd"], writes=["rstd"])

        def norm_to_xn(vcol):
            rms_stats(HALO, T)
            for kc in range(KC):
                P.add("dve", lambda h, kc=kc: h.scalar_tensor_tensor(
                    out=XNv[:, kc, :], in0=Xv[:, kc, HALO:TW], scalar=VECS[:, vcol + kc:vcol + kc + 1],
                    in1=RSTD[:, HALO:TW], op0=ALU.mult, op1=ALU.mult),
                      reads=[("x", kc), "rstd", "vecs"], writes=[("xn", kc)])

        def mlp(layer, vcol, nf):
            norm_to_xn(vcol)
            wu = w_up_d[layer]
            wd = w_down_d[layer]
            fch = 64 // nf
            for fs in range(nf):
                def evac_h(fl, pb, pk):
                    j = cnt["rl"] % 2
                    cnt["rl"] += 1
                    rl = RL[j]
                    P.add("act", lambda h, rl=rl, pb=pb: h.activation(out=rl[:], in_=pb[:], func=AF.Relu),
                          reads=[pk], writes=[("rl", j)])
                    P.add("dve", lambda h, rl=rl, fl=fl: h.tensor_tensor(out=Gv[:, fl, :], in0=rl[:], in1=rl[:],
                                                                         op=ALU.mult),
                          reads=[("rl", j)], writes=[("g", fl)])

                yield from proj_fm(wu, 0, fs * fch * 128, KC, fch // 4, lambda kc: XNv[:, kc, :],
                                   lambda kc: ("xn", kc), evac_h)

                def evac_y(dc, pb, pk):
                    P.add("dve", lambda h, pb=pb, dc=dc: h.tensor_tensor(
                        out=Xv[:, dc, HALO:TW], in0=pb[:], in1=Xv[:, dc, HALO:TW], op=ALU.add),
                          reads=[pk, ("x", dc)], writes=[("x", dc)])

                yield from proj_fm(wd, fs * fch * 128, 0, fch, 4, lambda fc: Gv[:, fc, :], lambda fc: ("g", fc),
                                   evac_y)

        def construct():
            finals = []
            P.add("sync", lambda h: h.dma_start(out=CST[:], in_=cst_d), writes=["cst"], dma=True, slot="cst")
            P.add("sync", lambda h: h.dma_start(out=VECS[:], in_=vecs_d), writes=["vecs"], dma=True, slot="vecs")
            P.add("dve", lambda h: h.memset(EPSB[:], EPS), writes=["eps"])
            xkeys = [("x", kc) for kc in range(KC)]

            P.add("sync", lambda h: h.dma_start(out=IC, in_=invcnt_d), writes=["ic"], dma=True, slot="ic")
            ICv = IC.rearrange("p (g c) -> p g c", g=4)
            xsrc = xT_d.rearrange("(k p) i c -> p k i c", p=128)
            x1v = x1T_d.rearrange("(k p) i c -> p k i c", p=128)
            GROUPS = [[0, 1], [2, 3], [4, 5], [6, 7]]
            GQ = 0
            GD = 16

            def x_load(i, eng):
                P.add(eng, lambda h, i=i: h.dma_start(out=Xv, in_=xsrc[:, :, i, :]),
                      writes=xkeys, dma=True, slot="x")

            def mixer(i):
                for kc in range(KC):
                    g = kc // 4
                    w = WINS[g]
                    e = "dve"
                    xnp = TMPv[0]
                    P.add(e, lambda h, kc=kc, xnp=xnp: h.scalar_tensor_tensor(
                        out=xnp, in0=Xv[:, kc, :], scalar=VECS[:, V_PN + kc:V_PN + kc + 1], in1=RSTD[:],
                        op0=ALU.mult, op1=ALU.mult),
                          reads=[("x", kc), "rstd", "vecs"], writes=[("tmp", 0)])
                    cur, curk = xnp, ("tmp", 0)
                    m = 2
                    lvl = 0
                    while m <= w:
                        dst = TMPv[1 + lvl % 2]
                        dk = ("tmp", 1 + lvl % 2)
                        hm = m // 2
                        P.add(e, lambda h, dst=dst, cur=cur, m=m, hm=hm: h.tensor_tensor(
                            out=dst[:, m - 1:TW], in0=cur[:, m - 1:TW], in1=cur[:, m - 1 - hm:TW - hm], op=ALU.add),
                              reads=[curk], writes=[dk])
                        cur, curk = dst, dk
                        m *= 2
                        lvl += 1
                    if i == 0:
                        t3 = TMPv[3]
                        P.add(e, lambda h, t3=t3, cur=cur, g=g: h.tensor_tensor(
                            out=t3[:, 0:T], in0=cur[:, HALO:TW], in1=ICv[:, g, :], op=ALU.mult),
                              reads=[curk, "ic"], writes=[("tmp", 3)])
                        P.add(e, lambda h, t3=t3, xnp=xnp, kc=kc: h.tensor_tensor(
                            out=Gv[:, GD + kc, :], in0=t3[:, 0:T], in1=xnp[:, HALO:TW], op=ALU.subtract),
                              reads=[("tmp", 3), ("tmp", 0)], writes=[("g", GD + kc)])
                    else:
                        P.add(e, lambda h, cur=cur, xnp=xnp, kc=kc, w=w: h.scalar_tensor_tensor(
                            out=Gv[:, GD + kc, :], in0=cur[:, HALO:TW], scalar=1.0 / w, in1=xnp[:, HALO:TW],
                            op0=ALU.mult, op1=ALU.subtract),
                              reads=[curk, ("tmp", 0)], writes=[("g", GD + kc)])

            def head_group_proj(w2d, col0, rhs_fn, rhs_key):
                banks = [bank() for _ in range(4)]
                for kq in range(4):
                    wv, wk = load_slab(wslab(w2d, kq * 512, col0, 4, 512), 4, 512)
                    for kl in range(4):
                        kc = kq * 4 + kl
                        for ol in range(4):
                            pb, pk = banks[ol]
                            P.add("pe", lambda h, wv=wv, pb=pb, kl=kl, ol=ol, kc=kc: h.matmul(
                                pb[:], lhsT=wv[:, kl, ol * 128:(ol + 1) * 128], rhs=rhs_fn(kc),
                                start=(kc == 0), stop=(kc == KC - 1)),
                                  reads=[wk, rhs_key(kc)], writes=[pk])
                return banks

            x_load(0, "sync")
            rms_stats(0, TW)
            mixer(0)
            for i in range(NT):
                for g in range(4):
                    def evac_p(ec, pb, pk, g=g):
                        dc = 4 * g + ec
                        P.add("dve", lambda h, pb=pb, dc=dc: h.scalar_tensor_tensor(
                            out=Xv[:, dc, HALO:TW], in0=pb[:], scalar=VECS[:, V_PS + dc:V_PS + dc + 1],
                            in1=Xv[:, dc, HALO:TW], op0=ALU.mult, op1=ALU.add),
                              reads=[pk, ("x", dc), "vecs"], writes=[("x", dc)])

                    run(proj_fm(pool_w_d[g], 0, 0, 4, 1, lambda cc, g=g: Gv[:, GD + 4 * g + cc, :],
                                lambda cc, g=g: ("g", GD + 4 * g + cc), evac_p))
                run(mlp(0, V_M0, 2))
                P.add("act", lambda h, i=i: h.dma_start(out=x1v[:, :, i, :], in_=Xv[:, :, HALO:TW]),
                      reads=xkeys, writes=[("x1d", i)], dma=True, slot="x1")
                rms_stats(HALO, T)
                for kc in range(KC):
                    P.add("dve", lambda h, kc=kc: h.scalar_tensor_tensor(
                        out=XNv[:, kc, :], in0=Xv[:, kc, HALO:TW], scalar=VECS[:, V_KV + kc:V_KV + kc + 1],
                        in1=RSTD[:, HALO:TW], op0=ALU.mult, op1=ALU.mult),
                          reads=[("x", kc), "rstd", "vecs"], writes=[("xn", kc)])
                for kc in range(KC):
                    P.add("dve", lambda h, kc=kc: h.scalar_tensor_tensor(
                        out=Gv[:, GQ + kc, :], in0=Xv[:, kc, HALO:TW], scalar=VECS[:, V_AN + kc:V_AN + kc + 1],
                        in1=RSTD[:, HALO:TW], op0=ALU.mult, op1=ALU.mult),
                          reads=[("x", kc), "rstd", "vecs"], writes=[("g", GQ + kc)])
                state["cast"] = "act"
                if i + 1 < NT:
                    x_load(i + 1, "sync")
                for hg in range(4):
                    kst = KST[hg % 2]
                    kstv = kst.rearrange("p (h c) -> p h c", h=4)
                    banks = head_group_proj(w_kv_d, hg * 512, lambda kc: XNv[:, kc, :], lambda kc: ("xn", kc))
                    for ol in range(4):
                        pb, pk = banks[ol]
                        P.add("act", lambda h, pb=pb, ol=ol, kstv=kstv: h.activation(
                            out=kstv[:, ol, :], in_=pb[:], func=AF.Copy),
                              reads=[pk], writes=[("kst", hg % 2)])
                    dst = kmine_t[i, hg].rearrange("(h p) c -> p h c", p=128)
                    P.add("act", lambda h, dst=dst, kstv=kstv: h.dma_start(out=dst, in_=kstv),
                          reads=[("kst", hg % 2)], writes=[("kmine", i, hg)], dma=True, slot=("kst", hg % 2))
                    P.add("pool", lambda h, i=i, hg=hg: h.collective_compute(
                        "AllGather", ALU.bypass, replica_groups=GROUPS, ins=[kmine_t[i, hg]], outs=[kall_t[i, hg]]),
                          reads=[("kmine", i, hg)], writes=[("kall", i, hg)], dma=True, slot="cc", inc=1)
                if i + 1 < NT:
                    rms_stats(0, TW)
                for hg in range(4):
                    vst = VST[hg % 2]
                    vstv = vst.rearrange("p (h t d) -> p h t d", h=4, t=4)
                    banks = [bank() for _ in range(4)]
                    for kq in range(4):
                        wv, wk = load_slab(wslab(w_kv_d, kq * 512, D + hg * 512, 4, 512), 4, 512)
                        for kl in range(4):
                            kc = kq * 4 + kl
                            for tt in range(4):
                                pb, pk = banks[tt]
                                P.add("pe", lambda h, wv=wv, pb=pb, kl=kl, kc=kc, tt=tt: h.matmul(
                                    pb[:], lhsT=XNv[:, kc, tt * 128:(tt + 1) * 128], rhs=wv[:, kl, :],
                                    start=(kc == 0), stop=(kc == KC - 1)),
                                      reads=[wk, ("xn", kc)], writes=[pk])
                    for tt in range(4):
                        pb, pk = banks[tt]
                        P.add("act", lambda h, pb=pb, vstv=vstv, tt=tt: h.activation(
                            out=vstv[:, :, tt, :], in_=pb[:].rearrange("p (h d) -> p h d", h=4), func=AF.Copy),
                              reads=[pk], writes=[("vst", hg % 2)])
                    dst = vmine_t[i, hg].rearrange("(h p) (t d) -> p h t d", p=128, t=4)
                    P.add("act", lambda h, dst=dst, vstv=vstv: h.dma_start(out=dst, in_=vstv),
                          reads=[("vst", hg % 2)], writes=[("vmine", i, hg)], dma=True, slot=("vst", hg % 2))
                    P.add("pool", lambda h, i=i, hg=hg: h.collective_compute(
                        "AllGather", ALU.bypass, replica_groups=GROUPS, ins=[vmine_t[i, hg]], outs=[vall_t[i, hg]]),
                          reads=[("vmine", i, hg)], writes=[("vall", i, hg)], dma=True, slot="cc", inc=1)
                    if hg == 0 and i + 1 < NT:
                        mixer(i + 1)
                for hg in range(4):
                    kst = KST[hg % 2]
                    kstv = kst.rearrange("p (h c) -> p h c", h=4)
                    banks = head_group_proj(w_q_d, hg * 512, lambda kc: Gv[:, GQ + kc, :],
                                            lambda kc: ("g", GQ + kc))
                    for ol in range(4):
                        pb, pk = banks[ol]
                        P.add("act", lambda h, pb=pb, ol=ol, kstv=kstv: h.activation(
                            out=kstv[:, ol, :], in_=pb[:], func=AF.Copy, scale=INV_SQRT),
                              reads=[pk], writes=[("kst", hg % 2)])
                    dst = qsp_t[i, hg].rearrange("(h p) c -> p h c", p=128)
                    P.add("act", lambda h, dst=dst, kstv=kstv: h.dma_start(out=dst, in_=kstv),
                          reads=[("kst", hg % 2)], writes=[("qsp", i, hg)], dma=True, slot=("kst", hg % 2))
                state["cast"] = "alt"

            a_ops = P.retire(["ic", ("tmp", 0), ("tmp", 1), ("tmp", 2), ("tmp", 3), ("kst", 0), ("kst", 1),
                              ("vst", 0), ("vst", 1)])
            bkeys = (["negm"] + [("q", h_) for h_ in range(H)] + [("o", h_) for h_ in range(H)]
                     + [("eb", 0)] + [("sp", k) for k in range(2)] + [("ab", k) for k in range(2)]
                     + [("s", k) for k in range(2)] + [("ktb", 1), ("vb", 1)])
            for k in bkeys:
                P.seed(k, a_ops)

            P.add("sync", lambda h: h.dma_start(out=NEGM, in_=negm_d), writes=["negm"], dma=True, slot="negm")
            NEGMv = NEGM.rearrange("p (m c) -> p m c", m=8)
            QTv = QT.rearrange("p (h c) -> p h c", h=H)
            OTv = OT.rearrange("p (h c) -> p h c", h=H)
            odst = outT_d.rearrange("(k p) i c -> p k i c", p=128)
            KTB = [G[:, 8192:12288], KTB1]
            VB = [G[:, 12288:16384].rearrange("p (t d) -> p t d", d=128), VB1.rearrange("p (t d) -> p t d", d=128)]
            kvk = [([("g", k) for k in range(16, 24)], [("g", k) for k in range(24, 32)]), ([], [])]
            ZB = [0, 1, 2]
            OB = 3
            state["banks"] = [4, 5, 6, 7]
            cnt["bank"] = 0

            def kv_load(i, hh):
                hb = hh % 2
                hg, hl = hh // 4, hh % 4
                nch = i + 1
                for r in range(2):
                    r0 = r * 512 + hl * 128
                    ktd = KTB[hb].rearrange("p (c r t) -> p c r t", r=2, t=T)[:, 0:nch, r, :]
                    kts = kall_t[0:nch, hg, r0:r0 + 128, :].rearrange("i p c -> p i c")
                    P.add("sync", lambda h, ktd=ktd, kts=kts: h.dma_start(out=ktd, in_=kts),
                          reads=[("kall", i2, hg) for i2 in range(nch)],
                          writes=[("ktb", hb)] + kvk[hb][0], dma=True, slot=("ktb", hb))
                    vd = VB[hb].rearrange("p (c r t) d -> p c r t d", r=2, t=4)[:, 0:nch, r, :, :]
                    vs = vall_t[0:nch, hg, r0:r0 + 128, :].rearrange("i p (t d) -> p i t d", t=4)
                    P.add("sync", lambda h, vd=vd, vs=vs: h.dma_start(out=vd, in_=vs),
                          reads=[("vall", i2, hg) for i2 in range(nch)],
                          writes=[("vb", hb)] + kvk[hb][1], dma=True, slot=("vb", hb))

            def attention(i):
                nk = 8 * i + 8
                pairs = [(hh, kt) for hh in range(H) for kt in range(nk - 1, -1, -1)]
                n = len(pairs)
                qsrc = qsp_t[i].rearrange("g (h p) c -> p g h c", p=128)
                P.add("sync", lambda h, qsrc=qsrc: h.dma_start(
                    out=QT.rearrange("p (g h c) -> p g h c", g=4, h=4), in_=qsrc),
                      reads=[("qsp", i, g_) for g_ in range(4)], writes=[("q", hh) for hh in range(H)],
                      dma=True, slot="q")
                kv_load(i, 0)
                kv_load(i, 1)

                def s1(idx):
                    hh, kt = pairs[idx]
                    hb = hh % 2
                    zb = ZB[idx % 3]
                    Z, zk = PS[zb], ("ps", zb)
                    masked = kt >= 8 * i
                    P.add("pe", lambda h: h.matmul(Z[:], lhsT=KTB[hb][:, kt * 128:(kt + 1) * 128], rhs=QTv[:, hh, :],
                                                   start=True, stop=True),
                          reads=[("ktb", hb), ("q", hh)], writes=[zk])
                    if masked:
                        m = kt - 8 * i
                        P.add("pe", lambda h: h.matmul(Z[:], lhsT=ident, rhs=NEGMv[:, m, :], start=False, stop=True),
                              reads=["negm", "cst", zk], writes=[zk])
                    eb = EB[0]
                    P.add("act", lambda h: h.activation(out=eb, in_=Z[:], func=AF.Exp),
                          reads=[zk], writes=[("eb", 0)])

                def s1b(idx):
                    eb = EB[0]
                    sp = SPB[idx % 2]
                    P.add("act", lambda h: h.activation(out=sp, in_=eb, func=AF.Ln, bias=1.0, scale=1.0),
                          reads=[("eb", 0)], writes=[("sp", idx % 2)])

                def s3(idx):
                    hh, kt = pairs[idx]
                    zb = ZB[idx % 3]
                    Z, zk = PS[zb], ("ps", zb)
                    sp = SPB[idx % 2]
                    first = kt == nk - 1
                    last = kt == 0
                    sprev = SB_[(idx - 1) % 2]
                    scur = SB_[idx % 2]
                    P.add("pe", lambda h: h.matmul(Z[:], lhsT=negU, rhs=sp, start=False, stop=True),
                          reads=[("sp", idx % 2), "cst", zk], writes=[zk])
                    if not first:
                        P.add("pe", lambda h: h.matmul(Z[:], lhsT=negones, rhs=sprev, start=False, stop=True),
                              reads=[("s", (idx - 1) % 2), "cst", zk], writes=[zk])
                    if not last:
                        if first:
                            P.add("pool", lambda h: h.tensor_copy(out=scur, in_=sp),
                                  reads=[("sp", idx % 2)], writes=[("s", idx % 2)])
                        else:
                            P.add("pool", lambda h: h.tensor_tensor(out=scur, in0=sprev, in1=sp, op=ALU.add),
                                  reads=[("sp", idx % 2), ("s", (idx - 1) % 2)], writes=[("s", idx % 2)])

                def s4(idx):
                    zb = ZB[idx % 3]
                    Z, zk = PS[zb], ("ps", zb)
                    ab = AB[idx % 2]
                    P.add("act", lambda h: h.activation(out=ab, in_=Z[:], func=AF.Exp),
                          reads=[zk], writes=[("ab", idx % 2)])

                def s5(idx):
                    hh, kt = pairs[idx]
                    hb = hh % 2
                    O, ok = PS[OB], ("ps", OB)
                    ab = AB[idx % 2]
                    P.add("pe", lambda h: h.matmul(O[:], lhsT=VB[hb][:, kt, :], rhs=ab,
                                                   start=(kt == nk - 1), stop=(kt == 0)),
                          reads=[("vb", hb), ("ab", idx % 2)], writes=[ok])
                    if kt == 0:
                        P.add("dve", lambda h: h.tensor_copy(out=OTv[:, hh, :], in_=O[:]),
                              reads=[ok], writes=[("o", hh)])
                        if hh + 2 < H:
                            kv_load(i, hh + 2)

                for step in range(n + 3):
                    if step < n:
                        s1(step)
                    if 0 <= step - 2 < n:
                        s4(step - 2)
                    if step < n:
                        s1b(step)
                    if 0 <= step - 1 < n:
                        s3(step - 1)
                    if 0 <= step - 3 < n:
                        s5(step - 3)
                    yield

            def foreground(i):
                def evac_o(dc, pb, pk):
                    P.add("dve", lambda h, pb=pb, dc=dc: h.tensor_tensor(
                        out=Xv[:, dc, HALO:TW], in0=pb[:], in1=Xv[:, dc, HALO:TW], op=ALU.add),
                          reads=[pk, ("x", dc)], writes=[("x", dc)])

                yield from proj_fm(w_o_d, 0, 0, H, 4, lambda hc: OTv[:, hc, :], lambda hc: ("o", hc), evac_o)
                yield from mlp(1, V_M1, 4)
                rms_stats(HALO, T)
                for kc in range(KC):
                    P.add("dve", lambda h, kc=kc: h.scalar_tensor_tensor(
                        out=Xv[:, kc, HALO:TW], in0=Xv[:, kc, HALO:TW], scalar=VECS[:, V_FN + kc:V_FN + kc + 1],
                        in1=RSTD[:, HALO:TW], op0=ALU.mult, op1=ALU.mult),
                          reads=[("x", kc), "rstd", "vecs"], writes=[("x", kc)])
                finals.append(P.add("act", lambda h, i=i: h.dma_start(out=odst[:, :, i, :], in_=Xv[:, :, HALO:TW]),
                                    reads=xkeys, dma=True, slot="out"))
                yield

            state["cast"] = "dve"
            run(attention(0))
            for i in range(NT):
                P.add("sync", lambda h, i=i: h.dma_start(out=Xv[:, :, HALO:TW], in_=x1v[:, :, i, :]),
                      reads=[("x1d", i)], writes=xkeys, dma=True, slot="x")
                fg = foreground(i)
                if i + 1 < NT:
                    bg = attention(i + 1)
                    n_bg = 16 * (8 * (i + 1) + 8) + 3
                    n_fg = 16 + 128 + 1
                    done_fg = 0
                    done_bg = 0
                    for _ in fg:
                        done_fg += 1
                        if done_fg <= 16:
                            continue
                        want = (n_bg * (done_fg - 16)) // (n_fg - 16 - 8)
                        while done_bg < min(want, n_bg):
                            try:
                                next(bg)
                            except StopIteration:
                                done_bg = n_bg
                                break
                            done_bg += 1
                    run(bg)
                else:
                    run(fg)

            return finals

        P_real = P
        P = Prog(nc)
        state["dry"] = True
        construct()
        P = P_real
        for k_ in cnt:
            cnt[k_] = 0
        state["dry"] = False
        state["issued"] = 0
        state["banks"] = list(range(8))
        state["cast"] = "alt"
        finals = construct()
        P.final_wait("act", finals)
        P.emit()
    return nc


def _vec_layout(v):
    return np.ascontiguousarray(np.asarray(v, np.float32).reshape(KC, 128).T)


def _consts():
    j = np.arange(128)
    ident = np.eye(128, dtype=np.float32)
    negU = -(j[:, None] >= j[None, :]).astype(np.float32)
    negones = -np.ones((128, 128), np.float32)
    meanones = np.full((128, 128), 1.0 / D, np.float32)
    return np.concatenate([ident, negU, negones, meanones], axis=1).astype(ml_dtypes.bfloat16)


_CACHE = {}


def _prog():
    if "p" not in _CACHE:
        _CACHE["p"] = build()
    return _CACHE["p"]


def kernel(x, pool_norm, pool_w, pool_scale, kv_norm, w_kv, attn_norm, w_q, w_o, mlp_norm, w_up, w_down,
           final_norm):
    x = np.asarray(x, np.float32)
    B = x.shape[0]
    n_cores = 8
    vecs = np.concatenate([
        _vec_layout(pool_norm[0]), _vec_layout(pool_scale[0]), _vec_layout(kv_norm), _vec_layout(attn_norm[0]),
        _vec_layout(mlp_norm[0]), _vec_layout(mlp_norm[1]), _vec_layout(final_norm)], axis=1)
    vecs = np.ascontiguousarray(vecs, np.float32)
    cst = _consts()
    w_up = np.ascontiguousarray(w_up, np.float32)
    w_down = np.ascontiguousarray(w_down, np.float32)
    pool_w3 = np.ascontiguousarray(np.asarray(pool_w, np.float32)[0])
    w_kv = np.ascontiguousarray(w_kv, np.float32)
    w_q2 = np.ascontiguousarray(np.asarray(w_q, np.float32)[0])
    w_o2 = np.ascontiguousarray(np.asarray(w_o, np.float32)[0])

    ins = []
    for c in range(n_cores):
        b, p = c // 2, c % 2
        xT = np.zeros((D, NT, TW), np.float32)
        for i in range(NT):
            j = 2 * i + p
            xT[:, i, HALO:] = x[b, j * T:(j + 1) * T, :].T
            if j > 0:
                xT[:, i, :HALO] = x[b, j * T - HALO:j * T, :].T
        pos = p * T + np.arange(T)
        ic = np.stack([1.0 / np.minimum(pos + 1, w) for w in WINS]).astype(np.float32)
        invcnt = np.ascontiguousarray(np.broadcast_to(ic.reshape(1, 4 * T), (128, 4 * T)))
        sk = np.arange(128)[:, None, None]
        m = np.arange(8)[None, :, None]
        tq = np.arange(T)[None, None, :]
        negm = np.where(m * 128 + sk >= p * T + tq, -30000.0, 0.0).astype(np.float32)
        negm = negm.reshape(128, 8 * T).astype(ml_dtypes.bfloat16)
        ins.append({"vecs": vecs, "cst": cst, "w_up": w_up, "w_down": w_down, "xT": xT, "invcnt": invcnt,
                    "pool_w": pool_w3, "w_kv": w_kv, "negm": negm, "w_q": w_q2, "w_o": w_o2})
    res = run_bass_kernel_spmd(_prog(), ins, core_ids=list(range(n_cores))).results

    out = np.empty((B, SEQ, D), np.float32)
    for c in range(n_cores):
        b, p = c // 2, c % 2
        oT = res[c]["outT"]
        for i in range(NT):
            j = 2 * i + p
            out[b, j * T:(j + 1) * T, :] = oT[:, i, :].T
    return out
```

```python
import contextlib
import numpy as np
import ml_dtypes
import concourse.bass as bass
import concourse.mybir as mybir
from concourse.bass_utils import run_bass_kernel_spmd

F32 = mybir.dt.float32
BF16 = mybir.dt.bfloat16
AF = mybir.ActivationFunctionType
ALU = mybir.AluOpType

D = 2048
KC = 16
T = 512
NT = 4
HALO = 16
TW = T + HALO
DFF = 8192
H = 16
SEQ = 4096
EPS = 1e-6
WINS = (2, 4, 8, 16)
NS = 3
NB = 3
LA = NB - 1
SLAB = 2048
INV_SQRT = 1.0 / float(np.sqrt(128.0))
V_PN, V_PS, V_KV, V_AN, V_M0, V_M1, V_FN = [16 * k for k in range(7)]


class Op:
    __slots__ = ("eng", "fn", "deps", "dma", "slot", "ev", "needed", "inc")

    def __init__(self, eng, fn, dma, slot, inc=16):
        self.eng = eng
        self.fn = fn
        self.dma = dma
        self.slot = slot
        self.inc = inc
        self.deps = ()
        self.ev = None
        self.needed = False


class Prog:
    ENGS = ("sync", "act", "dve", "pool", "pe")

    def __init__(self, nc):
        self.nc = nc
        self.ops = {e: [] for e in self.ENGS}
        self.last_w = {}
        self.readers = {}
        self.finals = []
        self.pending = {e: set() for e in self.ENGS}
        self.dma_since = []

    def retire(self, keys):
        ops = set()
        for k in keys:
            w = self.last_w.get(k)
            if w is not None:
                ops.add(w)
            ops.update(self.readers.get(k, ()))
        return ops

    def seed(self, key, ops):
        self.readers.setdefault(key, []).extend(ops)

    def add(self, eng, fn, reads=(), writes=(), dma=False, slot=None, inc=16):
        op = Op(eng, fn, dma, slot, inc)
        deps = set()
        for k in reads:
            w = self.last_w.get(k)
            if w is not None:
                deps.add(w)
        for k in writes:
            w = self.last_w.get(k)
            if w is not None:
                deps.add(w)
            for r in self.readers.get(k, ()):
                deps.add(r)
        if self.pending[eng]:
            deps |= self.pending[eng]
            self.pending[eng] = set()
        if eng == "pe" and not dma:
            deps = {d for d in deps if not (d.eng == "pe" and not d.dma)}
        op.deps = deps
        for d in deps:
            d.needed = True
        for k in reads:
            self.readers.setdefault(k, []).append(op)
        for k in writes:
            self.last_w[k] = op
            self.readers[k] = []
        self.ops[eng].append(op)
        if dma:
            self.dma_since.append(op)
        return op

    def barrier(self):
        deps = set(self.dma_since)
        for e in ("act", "dve", "pool", "pe"):
            for op in reversed(self.ops[e]):
                if not op.dma:
                    deps.add(op)
                    break
        for e in self.ENGS:
            self.pending[e] |= deps
        self.dma_since = []

    def final_wait(self, eng, ops):
        for o in ops:
            o.needed = True
        self.finals.append((eng, list(ops)))

    def emit(self):
        nc = self.nc
        slots = []
        for e in self.ENGS:
            for op in self.ops[e]:
                if op.dma and op.slot not in slots:
                    slots.append(op.slot)
        with contextlib.ExitStack() as st:
            esem = {}
            for e in ("act", "dve", "pool", "pe"):
                esem[e] = st.enter_context(nc.semaphore("s_" + e))
            ssem = {}
            for i, s in enumerate(slots):
                ssem[s] = st.enter_context(nc.semaphore("d_%d" % i))
            cnt = {e: 0 for e in esem}
            scnt = {s: 0 for s in slots}
            for e in self.ENGS:
                for op in self.ops[e]:
                    if op.dma:
                        scnt[op.slot] += op.inc
                        op.ev = (ssem[op.slot], scnt[op.slot])
                    elif op.needed:
                        cnt[e] += 1
                        op.ev = (esem[e], cnt[e])
            block = st.enter_context(nc.Block())
            handles = {
                "sync": block.sync,
                "act": block.scalar,
                "dve": block.vector,
                "pool": block.gpsimd,
                "pe": block.tensor,
            }

            def make(e):
                def body(h):
                    waited = {}

                    def wait_for(d):
                        sem, val = d.ev
                        if waited.get(id(sem), 0) < val:
                            h.wait_ge(sem, val)
                            waited[id(sem)] = val

                    for op in self.ops[e]:
                        for d in op.deps:
                            wait_for(d)
                        ins = op.fn(h)
                        if op.dma:
                            ins.then_inc(op.ev[0], op.inc)
                        elif op.needed:
                            ins.then_inc(op.ev[0], 1)
                    for (fe, fops) in self.finals:
                        if fe == e:
                            for d in fops:
                                wait_for(d)

                return body

            for e in self.ENGS:
                handles[e](make(e))


def build():
    nc = bass.Bass("TRN2", target_bir_lowering=False)
    P = Prog(nc)

    def din(name, shape, dt=F32):
        return nc.dram_tensor(name, list(shape), dt, kind="ExternalInput").ap()

    def dout(name, shape, dt=F32):
        return nc.dram_tensor(name, list(shape), dt, kind="ExternalOutput").ap()

    def dint(name, shape, dt=F32):
        return nc.dram_tensor(name, list(shape), dt)

    vecs_d = din("vecs", [128, 7 * 16])
    cst_d = din("cst", [128, 4 * 128], BF16)
    w_up_d = din("w_up", [2, D, DFF])
    w_down_d = din("w_down", [2, DFF, D])
    xT_d = din("xT", [D, NT, TW])
    invcnt_d = din("invcnt", [128, 4 * T])
    pool_w_d = din("pool_w", [4, 512, 512])
    w_kv_d = din("w_kv", [D, 2 * D])
    negm_d = din("negm", [128, 8 * T], BF16)
    w_q_d = din("w_q", [D, D])
    w_o_d = din("w_o", [D, D])
    outT_d = dout("outT", [D, NT, T])
    x1T_t = dint("x1T", [D, NT, T])
    kmine_t = dint("kmine", [NT, 4, 512, 512], BF16)
    vmine_t = dint("vmine", [NT, 4, 512, 512], BF16)
    kall_t = dint("kall", [NT, 4, 1024, 512], BF16)
    vall_t = dint("vall", [NT, 4, 1024, 512], BF16)
    qsp_t = dint("qsp", [NT, 4, 512, 512], BF16)
    x1T_d = x1T_t.ap()

    st = contextlib.ExitStack()
    with st:
        def sb(name, shape, dt):
            return st.enter_context(nc.sbuf_tensor(name, list(shape), dt))

        X = sb("X", [128, KC * TW], F32)
        XN = sb("XN", [128, KC * T], BF16)
        G = sb("G", [128, 32 * T], BF16)
        RSTD = sb("RSTD", [128, TW], F32)
        SQ = [sb("SQ%d" % k, [128, TW], BF16) for k in range(2)]
        RL = [sb("RL%d" % k, [128, T], BF16) for k in range(2)]
        STG = [sb("STG%d" % k, [128, SLAB], F32) for k in range(NS)]
        WB = [sb("WB%d" % k, [128, SLAB], BF16) for k in range(NB)]
        CST = sb("CST", [128, 4 * 128], BF16)
        VECS = sb("VECS", [128, 7 * 16], F32)
        EPSB = sb("EPSB", [128, 1], F32)
        PHN = 32768
        PH = sb("PH", [128, PHN], BF16)
        PHf = PH[:].bitcast(F32)

        def carve_bf(off, n):
            return PH[:, off:off + n]

        def carve_f32(off, n):
            return PHf[:, off // 2:(off + n) // 2]

        IC = carve_f32(0, 4096)
        TMPv = [carve_f32(4096 + k * 1056, 1056) for k in range(4)]
        KST = [carve_bf(8320 + k * 2048, 2048) for k in range(2)]
        VST = [carve_bf(12416 + k * 2048, 2048) for k in range(2)]
        NEGM = carve_bf(0, 4096)
        QT = carve_bf(4096, 8192)
        OT = carve_bf(12288, 8192)
        EB = [carve_f32(20480, 1024)]
        SPB = [carve_bf(21504 + k * 512, 512) for k in range(2)]
        AB = [carve_bf(22528 + k * 512, 512) for k in range(2)]
        SB_ = [carve_bf(23552 + k * 512, 512) for k in range(2)]
        KTB1 = carve_bf(24576, 4096)
        VB1 = carve_bf(28672, 4096)
        PS = [st.enter_context(nc.psum_tensor("PS%d" % k, [128, T], F32)) for k in range(8)]

        Xv = X[:].rearrange("p (k c) -> p k c", k=KC)
        XNv = XN[:].rearrange("p (k c) -> p k c", k=KC)
        Gv = G[:].rearrange("p (k c) -> p k c", k=32)
        ident = CST[:, 0:128]
        negU = CST[:, 128:256]
        negones = CST[:, 256:384]
        meanones = CST[:, 384:512]

        cnt = {"bank": 0, "slab": 0, "sq": 0, "rl": 0}

        def bank():
            pool_ = state["banks"]
            b = pool_[cnt["bank"] % len(pool_)]
            cnt["bank"] += 1
            return PS[b], ("ps", b)

        plan = []
        state = {"dry": True, "issued": 0, "banks": list(range(8)), "cast": "alt"}

        def issue_slab(n):
            src3, R, W = plan[n]
            s = n % NS
            b = n % NB
            stv = STG[s][:, 0:R * W]
            P.add("sync", lambda h: h.dma_start(out=stv.rearrange("p (r w) -> p r w", r=R), in_=src3),
                  writes=[("st", s)], dma=True, slot=("st", s))
            wbv = WB[b][:, 0:R * W]
            cm = state["cast"]
            if cm == "dve" or (cm == "alt" and n % 2 == 0):
                P.add("dve", lambda h: h.tensor_copy(out=wbv, in_=stv), reads=[("st", s)], writes=[("wb", b)])
            else:
                P.add("act", lambda h: h.activation(out=wbv, in_=stv, func=AF.Copy), reads=[("st", s)],
                      writes=[("wb", b)])

        def load_slab(src3, R, W):
            n = cnt["slab"]
            cnt["slab"] += 1
            b = n % NB
            if state["dry"]:
                plan.append((src3, R, W))
            else:
                while state["issued"] <= min(n + LA, len(plan) - 1):
                    issue_slab(state["issued"])
                    state["issued"] += 1
            return WB[b][:, 0:R * W].rearrange("p (r w) -> p r w", r=R), ("wb", b)

        def wslab(w2d, r0, c0, R, W):
            return w2d[r0:r0 + R * 128, c0:c0 + W].rearrange("(r p) w -> p r w", p=128)

        def proj_fm(w2d, row0, col0, kchunks, ngroups, rhs_fn, rhs_key, evac_fn):
            for og in range(ngroups):
                banks = [bank() for _ in range(4)]
                for kq in range(kchunks // 4):
                    wv, wk = load_slab(wslab(w2d, row0 + kq * 512, col0 + og * 512, 4, 512), 4, 512)
                    for kl in range(4):
                        kc = kq * 4 + kl
                        for ol in range(4):
                            pb, pk = banks[ol]
                            P.add("pe", lambda h, wv=wv, pb=pb, kl=kl, ol=ol, kc=kc: h.matmul(
                                pb[:], lhsT=wv[:, kl, ol * 128:(ol + 1) * 128], rhs=rhs_fn(kc),
                                start=(kc == 0), stop=(kc == kchunks - 1)),
                                  reads=[wk, rhs_key(kc)], writes=[pk])
                        if kc == kchunks - 1:
                            for ol in range(4):
                                evac_fn(og * 4 + ol, banks[ol][0], banks[ol][1])
                        yield

        def run(gen):
            for _ in gen:
                pass

        def rms_stats(c0, ncol):
            segs = []
            c = c0
            if c0 < HALO:
                segs.append((c0, HALO))
                c = HALO
            segs.append((c, c0 + ncol))
            banks = [bank() for _ in segs]
            for kc in range(KC):
                j = cnt["sq"] % 2
                cnt["sq"] += 1
                sq = SQ[j]
                P.add("act", lambda h, sq=sq, kc=kc: h.activation(out=sq[:, c0:c0 + ncol], in_=Xv[:, kc, c0:c0 + ncol],
                                                                  func=AF.Square),
                      reads=[("x", kc)], writes=[("sq", j)])
                for (a, b_), (pb, pk) in zip(segs, banks):
                    P.add("pe", lambda h, sq=sq, a=a, b_=b_, pb=pb, kc=kc: h.matmul(
                        pb[:, 0:b_ - a], lhsT=meanones, rhs=sq[:, a:b_], start=(kc == 0), stop=(kc == KC - 1)),
                          reads=[("sq", j), "cst"], writes=[pk])
            for (a, b_), (pb, pk) in zip(segs, banks):
                P.add("act", lambda h, a=a, b_=b_, pb=pb: h.activation(out=RSTD[:, a:b_], in_=pb[:, 0:b_ - a],
                                                                       func=AF.Sqrt, bias=EPSB[:], scale=1.0),
                      reads=[pk, "eps"], writes=["rstd"])
            P.add("dve", lambda h: h.reciprocal(out=RSTD[:, c0:c0 + ncol], in_=RSTD[:, c0:c0 + ncol]),
                  reads=["rstd"], writes=["rstd"])

        def norm_to_xn(vcol):
            rms_stats(HALO, T)
            for kc in range(KC):
                P.add("dve", lambda h, kc=kc: h.scalar_tensor_tensor(
                    out=XNv[:, kc, :], in0=Xv[:, kc, HALO:TW], scalar=VECS[:, vcol + kc:vcol + kc + 1],
                    in1=RSTD[:, HALO:TW], op0=ALU.mult, op1=ALU.mult),
                      reads=[("x", kc), "rstd", "vecs"], writes=[("xn", kc)])

        def mlp(layer, vcol, nf):
            norm_to_xn(vcol)
            wu = w_up_d[layer]
            wd = w_down_d[layer]
            fch = 64 // nf
            for fs in range(nf):
                def evac_h(fl, pb, pk):
                    j = cnt["rl"] % 2
                    cnt["rl"] += 1
                    rl = RL[j]
                    P.add("act", lambda h, rl=rl, pb=pb: h.activation(out=rl[:], in_=pb[:], func=AF.Relu),
                          reads=[pk], writes=[("rl", j)])
                    P.add("dve", lambda h, rl=rl, fl=fl: h.tensor_tensor(out=Gv[:, fl, :], in0=rl[:], in1=rl[:],
                                                                         op=ALU.mult),
                          reads=[("rl", j)], writes=[("g", fl)])

                yield from proj_fm(wu, 0, fs * fch * 128, KC, fch // 4, lambda kc: XNv[:, kc, :],
                                   lambda kc: ("xn", kc), evac_h)

                def evac_y(dc, pb, pk):
                    P.add("dve", lambda h, pb=pb, dc=dc: h.tensor_tensor(
                        out=Xv[:, dc, HALO:TW], in0=pb[:], in1=Xv[:, dc, HALO:TW], op=ALU.add),
                          reads=[pk, ("x", dc)], writes=[("x", dc)])

                yield from proj_fm(wd, fs * fch * 128, 0, fch, 4, lambda fc: Gv[:, fc, :], lambda fc: ("g", fc),
                                   evac_y)

        def construct():
            finals = []
            P.add("sync", lambda h: h.dma_start(out=CST[:], in_=cst_d), writes=["cst"], dma=True, slot="cst")
            P.add("sync", lambda h: h.dma_start(out=VECS[:], in_=vecs_d), writes=["vecs"], dma=True, slot="vecs")
            P.add("dve", lambda h: h.memset(EPSB[:], EPS), writes=["eps"])
            xkeys = [("x", kc) for kc in range(KC)]

            P.add("sync", lambda h: h.dma_start(out=IC, in_=invcnt_d), writes=["ic"], dma=True, slot="ic")
            ICv = IC.rearrange("p (g c) -> p g c", g=4)
            xsrc = xT_d.rearrange("(k p) i c -> p k i c", p=128)
            x1v = x1T_d.rearrange("(k p) i c -> p k i c", p=128)
            GROUPS = [[0, 1], [2, 3], [4, 5], [6, 7]]
            GQ = 0
            GD = 16

            def x_load(i, eng):
                P.add(eng, lambda h, i=i: h.dma_start(out=Xv, in_=xsrc[:, :, i, :]),
                      writes=xkeys, dma=True, slot="x")

            def mixer(i):
                for kc in range(KC):
                    g = kc // 4
                    w = WINS[g]
                    e = "dve"
                    xnp = TMPv[0]
                    P.add(e, lambda h, kc=kc, xnp=xnp: h.scalar_tensor_tensor(
                        out=xnp, in0=Xv[:, kc, :], scalar=VECS[:, V_PN + kc:V_PN + kc + 1], in1=RSTD[:],
                        op0=ALU.mult, op1=ALU.mult),
                          reads=[("x", kc), "rstd", "vecs"], writes=[("tmp", 0)])
                    cur, curk = xnp, ("tmp", 0)
                    m = 2
                    lvl = 0
                    while m <= w:
                        dst = TMPv[1 + lvl % 2]
                        dk = ("tmp", 1 + lvl % 2)
                        hm = m // 2
                        P.add(e, lambda h, dst=dst, cur=cur, m=m, hm=hm: h.tensor_tensor(
                            out=dst[:, m - 1:TW], in0=cur[:, m - 1:TW], in1=cur[:, m - 1 - hm:TW - hm], op=ALU.add),
                              reads=[curk], writes=[dk])
                        cur, curk = dst, dk
                        m *= 2
                        lvl += 1
                    if i == 0:
                        t3 = TMPv[3]
                        P.add(e, lambda h, t3=t3, cur=cur, g=g: h.tensor_tensor(
                            out=t3[:, 0:T], in0=cur[:, HALO:TW], in1=ICv[:, g, :], op=ALU.mult),
                              reads=[curk, "ic"], writes=[("tmp", 3)])
                        P.add(e, lambda h, t3=t3, xnp=xnp, kc=kc: h.tensor_tensor(
                            out=Gv[:, GD + kc, :], in0=t3[:, 0:T], in1=xnp[:, HALO:TW], op=ALU.subtract),
                              reads=[("tmp", 3), ("tmp", 0)], writes=[("g", GD + kc)])
                    else:
                        P.add(e, lambda h, cur=cur, xnp=xnp, kc=kc, w=w: h.scalar_tensor_tensor(
                            out=Gv[:, GD + kc, :], in0=cur[:, HALO:TW], scalar=1.0 / w, in1=xnp[:, HALO:TW],
                            op0=ALU.mult, op1=ALU.subtract),
                              reads=[curk, ("tmp", 0)], writes=[("g", GD + kc)])

            def head_group_proj(w2d, col0, rhs_fn, rhs_key):
                banks = [bank() for _ in range(4)]
                for kq in range(4):
                    wv, wk = load_slab(wslab(w2d, kq * 512, col0, 4, 512), 4, 512)
                    for kl in range(4):
                        kc = kq * 4 + kl
                        for ol in range(4):
                            pb, pk = banks[ol]
                            P.add("pe", lambda h, wv=wv, pb=pb, kl=kl, ol=ol, kc=kc: h.matmul(
                                pb[:], lhsT=wv[:, kl, ol * 128:(ol + 1) * 128], rhs=rhs_fn(kc),
                                start=(kc == 0), stop=(kc == KC - 1)),
                                  reads=[wk, rhs_key(kc)], writes=[pk])
                return banks

            x_load(0, "sync")
            rms_stats(0, TW)
            mixer(0)
            for i in range(NT):
                for g in range(4):
                    def evac_p(ec, pb, pk, g=g):
                        dc = 4 * g + ec
                        P.add("dve", lambda h, pb=pb, dc=dc: h.scalar_tensor_tensor(
                            out=Xv[:, dc, HALO:TW], in0=pb[:], scalar=VECS[:, V_PS + dc:V_PS + dc + 1],
                            in1=Xv[:, dc, HALO:TW], op0=ALU.mult, op1=ALU.add),
                              reads=[pk, ("x", dc), "vecs"], writes=[("x", dc)])

                    run(proj_fm(pool_w_d[g], 0, 0, 4, 1, lambda cc, g=g: Gv[:, GD + 4 * g + cc, :],
                                lambda cc, g=g: ("g", GD + 4 * g + cc), evac_p))
                run(mlp(0, V_M0, 2))
                P.add("act", lambda h, i=i: h.dma_start(out=x1v[:, :, i, :], in_=Xv[:, :, HALO:TW]),
                      reads=xkeys, writes=[("x1d", i)], dma=True, slot="x1")
                rms_stats(HALO, T)
                for kc in range(KC):
                    P.add("dve", lambda h, kc=kc: h.scalar_tensor_tensor(
                        out=XNv[:, kc, :], in0=Xv[:, kc, HALO:TW], scalar=VECS[:, V_KV + kc:V_KV + kc + 1],
                        in1=RSTD[:, HALO:TW], op0=ALU.mult, op1=ALU.mult),
                          reads=[("x", kc), "rstd", "vecs"], writes=[("xn", kc)])
                for kc in range(KC):
                    P.add("dve", lambda h, kc=kc: h.scalar_tensor_tensor(
                        out=Gv[:, GQ + kc, :], in0=Xv[:, kc, HALO:TW], scalar=VECS[:, V_AN + kc:V_AN + kc + 1],
                        in1=RSTD[:, HALO:TW], op0=ALU.mult, op1=ALU.mult),
                          reads=[("x", kc), "rstd", "vecs"], writes=[("g", GQ + kc)])
                state["cast"] = "act"
                for hg in range(4):
                    kst = KST[hg % 2]
                    kstv = kst.rearrange("p (h c) -> p h c", h=4)
                    banks = head_group_proj(w_kv_d, hg * 512, lambda kc: XNv[:, kc, :], lambda kc: ("xn", kc))
                    for ol in range(4):
                        pb, pk = banks[ol]
                        P.add("act", lambda h, pb=pb, ol=ol, kstv=kstv: h.activation(
                            out=kstv[:, ol, :], in_=pb[:], func=AF.Copy),
                              reads=[pk], writes=[("kst", hg % 2)])
                    dst = kmine_t[i, hg].rearrange("(h p) c -> p h c", p=128)
                    P.add("act", lambda h, dst=dst, kstv=kstv: h.dma_start(out=dst, in_=kstv),
                          reads=[("kst", hg % 2)], writes=[("kmine", i, hg)], dma=True, slot=("kst", hg % 2))
                    P.add("pool", lambda h, i=i, hg=hg: h.collective_compute(
                        "AllGather", ALU.bypass, replica_groups=GROUPS, ins=[kmine_t[i, hg]], outs=[kall_t[i, hg]]),
                          reads=[("kmine", i, hg)], writes=[("kall", i, hg)], dma=True, slot="cc", inc=1)
                    if hg == 0 and i + 1 < NT:
                        x_load(i + 1, "sync")
                if i + 1 < NT:
                    rms_stats(0, TW)
                for hg in range(4):
                    vst = VST[hg % 2]
                    vstv = vst.rearrange("p (h t d) -> p h t d", h=4, t=4)
                    banks = [bank() for _ in range(4)]
                    for kq in range(4):
                        wv, wk = load_slab(wslab(w_kv_d, kq * 512, D + hg * 512, 4, 512), 4, 512)
                        for kl in range(4):
                            kc = kq * 4 + kl
                            for tt in range(4):
                                pb, pk = banks[tt]
                                P.add("pe", lambda h, wv=wv, pb=pb, kl=kl, kc=kc, tt=tt: h.matmul(
                                    pb[:], lhsT=XNv[:, kc, tt * 128:(tt + 1) * 128], rhs=wv[:, kl, :],
                                    start=(kc == 0), stop=(kc == KC - 1)),
                                      reads=[wk, ("xn", kc)], writes=[pk])
                    for tt in range(4):
                        pb, pk = banks[tt]
                        P.add("act", lambda h, pb=pb, vstv=vstv, tt=tt: h.activation(
                            out=vstv[:, :, tt, :], in_=pb[:].rearrange("p (h d) -> p h d", h=4), func=AF.Copy),
                              reads=[pk], writes=[("vst", hg % 2)])
                    dst = vmine_t[i, hg].rearrange("(h p) (t d) -> p h t d", p=128, t=4)
                    P.add("act", lambda h, dst=dst, vstv=vstv: h.dma_start(out=dst, in_=vstv),
                          reads=[("vst", hg % 2)], writes=[("vmine", i, hg)], dma=True, slot=("vst", hg % 2))
                    P.add("pool", lambda h, i=i, hg=hg: h.collective_compute(
                        "AllGather", ALU.bypass, replica_groups=GROUPS, ins=[vmine_t[i, hg]], outs=[vall_t[i, hg]]),
                          reads=[("vmine", i, hg)], writes=[("vall", i, hg)], dma=True, slot="cc", inc=1)
                    if hg == 0 and i + 1 < NT:
                        mixer(i + 1)
                for hg in range(4):
                    kst = KST[hg % 2]
                    kstv = kst.rearrange("p (h c) -> p h c", h=4)
                    banks = head_group_proj(w_q_d, hg * 512, lambda kc: Gv[:, GQ + kc, :],
                                            lambda kc: ("g", GQ + kc))
                    for ol in range(4):
                        pb, pk = banks[ol]
                        P.add("act", lambda h, pb=pb, ol=ol, kstv=kstv: h.activation(
                            out=kstv[:, ol, :], in_=pb[:], func=AF.Copy, scale=INV_SQRT),
                              reads=[pk], writes=[("kst", hg % 2)])
                    dst = qsp_t[i, hg].rearrange("(h p) c -> p h c", p=128)
                    P.add("act", lambda h, dst=dst, kstv=kstv: h.dma_start(out=dst, in_=kstv),
                          reads=[("kst", hg % 2)], writes=[("qsp", i, hg)], dma=True, slot=("kst", hg % 2))
                state["cast"] = "alt"

            a_ops = P.retire(["ic", ("tmp", 0), ("tmp", 1), ("tmp", 2), ("tmp", 3), ("kst", 0), ("kst", 1),
                              ("vst", 0), ("vst", 1)])
            bkeys = (["negm"] + [("q", h_) for h_ in range(H)] + [("o", h_) for h_ in range(H)]
                     + [("eb", 0)] + [("sp", k) for k in range(2)] + [("ab", k) for k in range(2)]
                     + [("s", k) for k in range(2)] + [("ktb", 1), ("vb", 1)])
            for k in bkeys:
                P.seed(k, a_ops)

            P.add("sync", lambda h: h.dma_start(out=NEGM, in_=negm_d), writes=["negm"], dma=True, slot="negm")
            NEGMv = NEGM.rearrange("p (m c) -> p m c", m=8)
            QTv = QT.rearrange("p (h c) -> p h c", h=H)
            OTv = OT.rearrange("p (h c) -> p h c", h=H)
            odst = outT_d.rearrange("(k p) i c -> p k i c", p=128)
            KTB = [G[:, 8192:12288], KTB1]
            VB = [G[:, 12288:16384].rearrange("p (t d) -> p t d", d=128), VB1.rearrange("p (t d) -> p t d", d=128)]
            kvk = [([("g", k) for k in range(16, 24)], [("g", k) for k in range(24, 32)]), ([], [])]
            ZB = [0, 1, 2]
            OB = 3
            state["banks"] = [4, 5, 6, 7]
            cnt["bank"] = 0

            def kv_load(i, hh):
                hb = hh % 2
                hg, hl = hh // 4, hh % 4
                nch = i + 1
                for r in range(2):
                    r0 = r * 512 + hl * 128
                    ktd = KTB[hb].rearrange("p (c r t) -> p c r t", r=2, t=T)[:, 0:nch, r, :]
                    kts = kall_t[0:nch, hg, r0:r0 + 128, :].rearrange("i p c -> p i c")
                    P.add("sync", lambda h, ktd=ktd, kts=kts: h.dma_start(out=ktd, in_=kts),
                          reads=[("kall", i2, hg) for i2 in range(nch)],
                          writes=[("ktb", hb)] + kvk[hb][0], dma=True, slot=("ktb", hb))
                    vd = VB[hb].rearrange("p (c r t) d -> p c r t d", r=2, t=4)[:, 0:nch, r, :, :]
                    vs = vall_t[0:nch, hg, r0:r0 + 128, :].rearrange("i p (t d) -> p i t d", t=4)
                    P.add("sync", lambda h, vd=vd, vs=vs: h.dma_start(out=vd, in_=vs),
                          reads=[("vall", i2, hg) for i2 in range(nch)],
                          writes=[("vb", hb)] + kvk[hb][1], dma=True, slot=("vb", hb))

            def attention(i):
                nk = 8 * i + 8
                pairs = [(hh, kt) for hh in range(H) for kt in range(nk - 1, -1, -1)]
                n = len(pairs)
                qsrc = qsp_t[i].rearrange("g (h p) c -> p g h c", p=128)
                P.add("sync", lambda h, qsrc=qsrc: h.dma_start(
                    out=QT.rearrange("p (g h c) -> p g h c", g=4, h=4), in_=qsrc),
                      reads=[("qsp", i, g_) for g_ in range(4)], writes=[("q", hh) for hh in range(H)],
                      dma=True, slot="q")
                kv_load(i, 0)
                kv_load(i, 1)

                def s1(idx):
                    hh, kt = pairs[idx]
                    hb = hh % 2
                    zb = ZB[idx % 3]
                    Z, zk = PS[zb], ("ps", zb)
                    masked = kt >= 8 * i
                    P.add("pe", lambda h: h.matmul(Z[:], lhsT=KTB[hb][:, kt * 128:(kt + 1) * 128], rhs=QTv[:, hh, :],
                                                   start=True, stop=True),
                          reads=[("ktb", hb), ("q", hh)], writes=[zk])
                    if masked:
                        m = kt - 8 * i
                        P.add("pe", lambda h: h.matmul(Z[:], lhsT=ident, rhs=NEGMv[:, m, :], start=False, stop=True),
                              reads=["negm", "cst", zk], writes=[zk])
                    eb = EB[0]
                    P.add("act", lambda h: h.activation(out=eb, in_=Z[:], func=AF.Exp),
                          reads=[zk], writes=[("eb", 0)])

                def s1b(idx):
                    eb = EB[0]
                    sp = SPB[idx % 2]
                    P.add("act", lambda h: h.activation(out=sp, in_=eb, func=AF.Ln, bias=1.0, scale=1.0),
                          reads=[("eb", 0)], writes=[("sp", idx % 2)])

                def s3(idx):
                    hh, kt = pairs[idx]
                    zb = ZB[idx % 3]
                    Z, zk = PS[zb], ("ps", zb)
                    sp = SPB[idx % 2]
                    first = kt == nk - 1
                    last = kt == 0
                    sprev = SB_[(idx - 1) % 2]
                    scur = SB_[idx % 2]
                    P.add("pe", lambda h: h.matmul(Z[:], lhsT=negU, rhs=sp, start=False, stop=True),
                          reads=[("sp", idx % 2), "cst", zk], writes=[zk])
                    if not first:
                        P.add("pe", lambda h: h.matmul(Z[:], lhsT=negones, rhs=sprev, start=False, stop=True),
                              reads=[("s", (idx - 1) % 2), "cst", zk], writes=[zk])
                    if not last:
                        if first:
                            P.add("pool", lambda h: h.tensor_copy(out=scur, in_=sp),
                                  reads=[("sp", idx % 2)], writes=[("s", idx % 2)])
                        else:
                            P.add("pool", lambda h: h.tensor_tensor(out=scur, in0=sprev, in1=sp, op=ALU.add),
                                  reads=[("sp", idx % 2), ("s", (idx - 1) % 2)], writes=[("s", idx % 2)])

                def s4(idx):
                    zb = ZB[idx % 3]
                    Z, zk = PS[zb], ("ps", zb)
                    ab = AB[idx % 2]
                    P.add("act", lambda h: h.activation(out=ab, in_=Z[:], func=AF.Exp),
                          reads=[zk], writes=[("ab", idx % 2)])

                def s5(idx):
                    hh, kt = pairs[idx]
                    hb = hh % 2
                    O, ok = PS[OB], ("ps", OB)
                    ab = AB[idx % 2]
                    P.add("pe", lambda h: h.matmul(O[:], lhsT=VB[hb][:, kt, :], rhs=ab,
                                                   start=(kt == nk - 1), stop=(kt == 0)),
                          reads=[("vb", hb), ("ab", idx % 2)], writes=[ok])
                    if kt == 0:
                        P.add("dve", lambda h: h.tensor_copy(out=OTv[:, hh, :], in_=O[:]),
                              reads=[ok], writes=[("o", hh)])
                        if hh + 2 < H:
                            kv_load(i, hh + 2)

                for step in range(n + 3):
                    if step < n:
                        s1(step)
                    if 0 <= step - 2 < n:
                        s4(step - 2)
                    if step < n:
                        s1b(step)
                    if 0 <= step - 1 < n:
                        s3(step - 1)
                    if 0 <= step - 3 < n:
                        s5(step - 3)
                    yield

            def foreground(i):
                def evac_o(dc, pb, pk):
                    P.add("dve", lambda h, pb=pb, dc=dc: h.tensor_tensor(
                        out=Xv[:, dc, HALO:TW], in0=pb[:], in1=Xv[:, dc, HALO:TW], op=ALU.add),
                          reads=[pk, ("x", dc)], writes=[("x", dc)])

                yield from proj_fm(w_o_d, 0, 0, H, 4, lambda hc: OTv[:, hc, :], lambda hc: ("o", hc), evac_o)
                yield from mlp(1, V_M1, 4)
                rms_stats(HALO, T)
                for kc in range(KC):
                    P.add("dve", lambda h, kc=kc: h.scalar_tensor_tensor(
                        out=Xv[:, kc, HALO:TW], in0=Xv[:, kc, HALO:TW], scalar=VECS[:, V_FN + kc:V_FN + kc + 1],
                        in1=RSTD[:, HALO:TW], op0=ALU.mult, op1=ALU.mult),
                          reads=[("x", kc), "rstd", "vecs"], writes=[("x", kc)])
                finals.append(P.add("act", lambda h, i=i: h.dma_start(out=odst[:, :, i, :], in_=Xv[:, :, HALO:TW]),
                                    reads=xkeys, dma=True, slot="out"))
                yield

            state["cast"] = "dve"
            run(attention(0))
            def x_reload(i):
                P.add("sync", lambda h, i=i: h.dma_start(out=Xv[:, :, HALO:TW], in_=x1v[:, :, i, :]),
                      reads=[("x1d", i)], writes=xkeys, dma=True, slot="x")

            RHOLD = 32
            x_reload(0)
            for i in range(NT):
                fg = foreground(i)
                if i + 1 < NT:
                    bg = attention(i + 1)
                    n_bg = 16 * (8 * (i + 1) + 8) + 3
                    n_fg = (16 + 128) * 4 + 1
                    j0 = 16 * 4
                    done_fg = 0
                    done_bg = 0
                    for _ in fg:
                        done_fg += 1
                        if done_fg <= j0:
                            continue
                        want = ((n_bg - RHOLD) * (done_fg - j0)) // (n_fg - j0 - 4)
                        while done_bg < min(want, n_bg - RHOLD):
                            try:
                                next(bg)
                            except StopIteration:
                                done_bg = n_bg
                                break
                            done_bg += 1
                    x_reload(i + 1)
                    run(bg)
                else:
                    run(fg)

            return finals

        P_real = P
        P = Prog(nc)
        state["dry"] = True
        construct()
        P = P_real
        for k_ in cnt:
            cnt[k_] = 0
        state["dry"] = False
        state["issued"] = 0
        state["banks"] = list(range(8))
        state["cast"] = "alt"
        finals = construct()
        P.final_wait("act", finals)
        P.emit()
    return nc


def _vec_layout(v):
    return np.ascontiguousarray(np.asarray(v, np.float32).reshape(KC, 128).T)


def _consts():
    j = np.arange(128)
    ident = np.eye(128, dtype=np.float32)
    negU = -(j[:, None] >= j[None, :]).astype(np.float32)
    negones = -np.ones((128, 128), np.float32)
    meanones = np.full((128, 128), 1.0 / D, np.float32)
    return np.concatenate([ident, negU, negones, meanones], axis=1).astype(ml_dtypes.bfloat16)


_CACHE = {}


def _prog():
    if "p" not in _CACHE:
        _CACHE["p"] = build()
    return _CACHE["p"]


def kernel(x, pool_norm, pool_w, pool_scale, kv_norm, w_kv, attn_norm, w_q, w_o, mlp_norm, w_up, w_down,
           final_norm):
    x = np.asarray(x, np.float32)
    B = x.shape[0]
    n_cores = 8
    vecs = np.concatenate([
        _vec_layout(pool_norm[0]), _vec_layout(pool_scale[0]), _vec_layout(kv_norm), _vec_layout(attn_norm[0]),
        _vec_layout(mlp_norm[0]), _vec_layout(mlp_norm[1]), _vec_layout(final_norm)], axis=1)
    vecs = np.ascontiguousarray(vecs, np.float32)
    cst = _consts()
    w_up = np.ascontiguousarray(w_up, np.float32)
    w_down = np.ascontiguousarray(w_down, np.float32)
    pool_w3 = np.ascontiguousarray(np.asarray(pool_w, np.float32)[0])
    w_kv = np.ascontiguousarray(w_kv, np.float32)
    w_q2 = np.ascontiguousarray(np.asarray(w_q, np.float32)[0])
    w_o2 = np.ascontiguousarray(np.asarray(w_o, np.float32)[0])

    ins = []
    for c in range(n_cores):
        b, p = c // 2, c % 2
        xT = np.zeros((D, NT, TW), np.float32)
        for i in range(NT):
            j = 2 * i + p
            xT[:, i, HALO:] = x[b, j * T:(j + 1) * T, :].T
            if j > 0:
                xT[:, i, :HALO] = x[b, j * T - HALO:j * T, :].T
        pos = p * T + np.arange(T)
        ic = np.stack([1.0 / np.minimum(pos + 1, w) for w in WINS]).astype(np.float32)
        invcnt = np.ascontiguousarray(np.broadcast_to(ic.reshape(1, 4 * T), (128, 4 * T)))
        sk = np.arange(128)[:, None, None]
        m = np.arange(8)[None, :, None]
        tq = np.arange(T)[None, None, :]
        negm = np.where(m * 128 + sk >= p * T + tq, -30000.0, 0.0).astype(np.float32)
        negm = negm.reshape(128, 8 * T).astype(ml_dtypes.bfloat16)
        ins.append({"vecs": vecs, "cst": cst, "w_up": w_up, "w_down": w_down, "xT": xT, "invcnt": invcnt,
                    "pool_w": pool_w3, "w_kv": w_kv, "negm": negm, "w_q": w_q2, "w_o": w_o2})
    res = run_bass_kernel_spmd(_prog(), ins, core_ids=list(range(n_cores))).results

    out = np.empty((B, SEQ, D), np.float32)
    for c in range(n_cores):
        b, p = c // 2, c % 2
        oT = res[c]["outT"]
        for i in range(NT):
            j = 2 * i + p
            out[b, j * T:(j + 1) * T, :] = oT[:, i, :].T
    return out
```
